# Optimizing a Trainium2 kernel written in Bass

```python
import math
import jax, jax.numpy as jnp
from jax import lax
import numpy as np

D_MODEL = 1024
BATCH = 16
SEQ = 4096
DEPTH = 4
DEC_BATCH = 4
DEC_SEQ = 8192
PAST_LEN = 128

N_HEADS = 8
N_KV_HEADS = 2
HEAD_DIM = 64
ATT_WIDTH = N_HEADS * HEAD_DIM
KV_WIDTH = N_KV_HEADS * HEAD_DIM
WINDOW = 128
BLOCK = 128
ROPE_DIM = HEAD_DIM // 4
ROPE_THETA = 500000.0
F_GROUPS = 4
F_GROUP_DIM = 128
F_WIDTH = F_GROUPS * F_GROUP_DIM
PLE_DIM = 256
EPS = 1e-6

SPLITS = [ATT_WIDTH, KV_WIDTH, KV_WIDTH, ATT_WIDTH, F_WIDTH, F_WIDTH, D_MODEL, D_MODEL]
IN_WIDTH = sum(SPLITS)
SPLIT_IDX = list(np.cumsum(SPLITS)[:-1])

kernel_name = "hybrid_window_gqa_fnet_gated_encoder"


def rmsnorm(x, g):
    xf = x.astype(jnp.float32)
    y = xf * lax.rsqrt(jnp.mean(xf * xf, axis=-1, keepdims=True) + EPS)
    return (y * g.astype(jnp.float32)).astype(x.dtype)


def partial_rope(x, pos):
    half = ROPE_DIM // 2
    inv_freq = ROPE_THETA ** (-jnp.arange(0, ROPE_DIM, 2, dtype=jnp.float32) / ROPE_DIM)
    ang = pos.astype(jnp.float32)[:, None] * inv_freq[None, :]
    cos = jnp.cos(ang)[None, :, None, :].astype(x.dtype)
    sin = jnp.sin(ang)[None, :, None, :].astype(x.dtype)
    x1 = x[..., :half]
    x2 = x[..., half:ROPE_DIM]
    return jnp.concatenate([x1 * cos - x2 * sin, x2 * cos + x1 * sin, x[..., ROPE_DIM:]], axis=-1)


def band_attention(q, k, v, sink):
    B, S = q.shape[0], q.shape[1]
    nb = S // BLOCK
    R = N_HEADS // N_KV_HEADS
    qb = q.reshape(B, nb, BLOCK, N_KV_HEADS, R, HEAD_DIM)
    pad = ((0, 0), (BLOCK, BLOCK), (0, 0), (0, 0))
    kp = jnp.pad(k, pad).reshape(B, nb + 2, BLOCK, N_KV_HEADS, HEAD_DIM)
    vp = jnp.pad(v, pad).reshape(B, nb + 2, BLOCK, N_KV_HEADS, HEAD_DIM)
    kw = jnp.concatenate([kp[:, :-2], kp[:, 1:-1], kp[:, 2:]], axis=2)
    vw = jnp.concatenate([vp[:, :-2], vp[:, 1:-1], vp[:, 2:]], axis=2)
    s = jnp.einsum('bnqgrd,bnkgd->bngrqk', qb, kw).astype(jnp.float32) * (1.0 / math.sqrt(HEAD_DIM))
    qi = jnp.arange(BLOCK)[:, None]
    ki = jnp.arange(3 * BLOCK)[None, :]
    rel = ki - BLOCK - qi
    jpos = jnp.arange(nb)[:, None, None] * BLOCK + (ki - BLOCK)[None]
    valid = (jnp.abs(rel)[None] <= WINDOW) & (jpos >= 0) & (jpos < S)
    s = jnp.where(valid[None, :, None, None], s, -1e30)
    sk = sink.astype(jnp.float32).reshape(N_KV_HEADS, R)[None, None, :, :, None, None]
    m = jnp.maximum(jnp.max(s, axis=-1, keepdims=True), sk)
    e = jnp.exp(s - m)
    pr = (e / (jnp.sum(e, axis=-1, keepdims=True) + jnp.exp(sk - m))).astype(v.dtype)
    o = jnp.einsum('bngrqk,bnkgd->bnqgrd', pr, vw)
    return o.reshape(B, S, ATT_WIDTH)


def fourier_mix(u, w_fmix):
    B, S = u.shape[0], u.shape[1]
    ug = u.reshape(B, S, F_GROUPS, F_GROUP_DIM).astype(jnp.float32)
    f = jnp.real(jnp.fft.fft2(ug, axes=(1, 3), norm='ortho')).astype(u.dtype)
    y = jnp.einsum('bsgc,gcd->bsgd', f, w_fmix)
    return y.reshape(B, S, F_WIDTH)


def trunk(x, p, ln1, w_in, sink, w_fmix, w_ao, w_fo, w_out, w_pe, ln_pg, w_pg, ln_f):
    B, S, _ = x.shape
    pos = jnp.arange(S)
    h = x
    for i in range(DEPTH):
        hn = rmsnorm(h, ln1[i])
        z = hn @ w_in[i]
        q, k, v, ga, uf, gf, mga, mgf = jnp.split(z, SPLIT_IDX, axis=-1)
        q = partial_rope(q.reshape(B, S, N_HEADS, HEAD_DIM), pos)
        k = partial_rope(k.reshape(B, S, N_KV_HEADS, HEAD_DIM), pos)
        v = v.reshape(B, S, N_KV_HEADS, HEAD_DIM)
        a = band_attention(q, k, v, sink[i]) * jax.nn.silu(ga)
        f = fourier_mix(uf, w_fmix[i]) * jax.nn.silu(gf)
        merged = jax.nn.sigmoid(mga) * (a @ w_ao[i]) + jax.nn.sigmoid(mgf) * (f @ w_fo[i])
        h = h + merged @ w_out[i]
        gate = jax.nn.sigmoid(rmsnorm(h, ln_pg[i]) @ w_pg[i])
        h = h + (p[i].astype(h.dtype) @ w_pe[i]) * gate
    return rmsnorm(h, ln_f)


def setup_inputs(seed: int = 0) -> dict:
    key = jax.random.key(seed)
    ks = jax.random.split(key, 16)
    f32 = jnp.float32
    nrm = lambda k, shape, scale: jax.random.normal(k, shape, f32) * scale
    return {
        "x_prompt": nrm(ks[0], (BATCH, SEQ, D_MODEL), 1.0),
        "x_sample": nrm(ks[1], (DEC_BATCH, DEC_SEQ, D_MODEL), 1.0),
        "p_prompt": nrm(ks[2], (DEPTH, BATCH, SEQ, PLE_DIM), 1.0),
        "p_sample": nrm(ks[3], (DEPTH, DEC_BATCH, DEC_SEQ, PLE_DIM), 1.0),
        "ln1": 1.0 + nrm(ks[4], (DEPTH, D_MODEL), 0.01),
        "w_in": nrm(ks[5], (DEPTH, D_MODEL, IN_WIDTH), D_MODEL ** -0.5),
        "sink": nrm(ks[6], (DEPTH, N_HEADS), 0.5),
        "w_fmix": nrm(ks[7], (DEPTH, F_GROUPS, F_GROUP_DIM, F_GROUP_DIM), F_GROUP_DIM ** -0.5),
        "w_ao": nrm(ks[8], (DEPTH, ATT_WIDTH, D_MODEL), ATT_WIDTH ** -0.5),
        "w_fo": nrm(ks[9], (DEPTH, F_WIDTH, D_MODEL), F_WIDTH ** -0.5),
        "w_out": nrm(ks[10], (DEPTH, D_MODEL, D_MODEL), 0.5 * D_MODEL ** -0.5),
        "w_pe": nrm(ks[11], (DEPTH, PLE_DIM, D_MODEL), 0.5 * PLE_DIM ** -0.5),
        "ln_pg": 1.0 + nrm(ks[12], (DEPTH, D_MODEL), 0.01),
        "w_pg": nrm(ks[13], (DEPTH, D_MODEL, D_MODEL), D_MODEL ** -0.5),
        "ln_f": 1.0 + nrm(ks[14], (D_MODEL,), 0.01),
    }


def reference(x_prompt, x_sample, p_prompt, p_sample, ln1, w_in, sink, w_fmix, w_ao, w_fo,
              w_out, w_pe, ln_pg, w_pg, ln_f):
    y_prompt = trunk(x_prompt, p_prompt, ln1, w_in, sink, w_fmix, w_ao, w_fo, w_out, w_pe, ln_pg, w_pg, ln_f)
    y_sample = trunk(x_sample, p_sample, ln1, w_in, sink, w_fmix, w_ao, w_fo, w_out, w_pe, ln_pg, w_pg, ln_f)
    return (y_prompt, y_sample)
```

```python
import numpy as np
import ml_dtypes
from contextlib import ExitStack

import concourse.bass as bass
import concourse.mybir as mybir
from concourse.bass_utils import run_bass_kernel_spmd

F32 = mybir.dt.float32
BF16 = mybir.dt.bfloat16
AF = mybir.ActivationFunctionType
ALU = mybir.AluOpType
AX = mybir.AxisListType
BF = ml_dtypes.bfloat16

D = 1024
IN_W = 4352
PLE = 256
EPS = 1e-6
C_Q, C_K, C_V, C_GA, C_UF, C_GF, C_MGA, C_MGF = 0, 512, 640, 768, 1280, 1792, 2304, 3328

T_ID = 0
T_CS = 128
T_F1 = 384
T_MASK = T_F1 + 768
TB16_W = T_MASK + 512


class Slot:
    __slots__ = ("w", "r", "name", "excl")

    def __init__(self, name="", excl=False):
        self.w = {}
        self.r = {}
        self.name = name
        self.excl = excl


class DSem:
    __slots__ = ("prog", "sub")

    def __init__(self, prog):
        self.prog = prog
        self.sub = {}

    def get(self, e):
        if e not in self.sub:
            self.sub[e] = [self.prog.new_sem(), 0]
        return self.sub[e]


class Prog:
    ENGS = ("pe", "act", "dve", "pool", "sp")

    def __init__(self, nc, es):
        self.nc = nc
        self.es = es
        self.esem = {e: es.enter_context(nc.semaphore("es_" + e)) for e in self.ENGS}
        self.cnt = {e: 0 for e in self.ENGS}
        self.waited = {e: {} for e in self.ENGS}
        self.ops = {e: [] for e in self.ENGS}
        self.dsems = []
        self.nsem = 0
        self.free = {e: 0.0 for e in self.ENGS}
        self.fin = {}
        self.step_fin = 0.0

    def new_sem(self):
        self.nsem += 1
        return self.es.enter_context(self.nc.semaphore("ds%d" % self.nsem))

    def dsem(self):
        d = DSem(self)
        self.dsems.append(d)
        return d

    def op(self, e, fn, reads=(), writes=(), dsem=None, ndma=1, n=512, cost=None):
        deps = {}
        own = id(self.esem[e])
        reads = [x for s_ in reads for x in (s_ if isinstance(s_, (list, tuple)) else [s_])]
        writes = [x for s_ in writes for x in (s_ if isinstance(s_, (list, tuple)) else [s_])]

        def need(key, sem, val):
            if key == own and e == "pe" and dsem is None:
                return
            if self.waited[e].get(key, 0) >= val:
                return
            if key not in deps or deps[key][1] < val:
                deps[key] = (sem, val)

        for s in reads:
            for key, (sem, val) in s.w.items():
                need(key, sem, val)
            if s.excl:
                for key, (sem, val) in s.r.items():
                    if key != own:
                        need(key, sem, val)
        for s in writes:
            for key, (sem, val) in s.w.items():
                need(key, sem, val)
            for key, (sem, val) in s.r.items():
                need(key, sem, val)
        t_start = self.free[e]
        for key, (sem, val) in deps.items():
            self.waited[e][key] = val
            t_start = max(t_start, self.fin.get((key, val), 0.0) + 0.12)
        if cost is None:
            if dsem is not None:
                cost = 2.5
            elif e == "pe":
                cost = 1.0
            elif e == "act":
                cost = 0.22 + n / 1200.0
            else:
                cost = 0.15 + n / 900.0
        t_fin = t_start + cost
        self.free[e] = (t_start + 0.1 * ndma) if dsem is not None else t_fin
        self.step_fin = max(self.step_fin, t_fin)
        if dsem is None:
            self.cnt[e] += 1
            ev = (self.esem[e], self.cnt[e])
            inc = 1
        else:
            sub = dsem.get(e)
            sub[1] += 16 * ndma
            ev = (sub[0], sub[1])
            inc = 16
        k = id(ev[0])
        self.fin[(k, ev[1])] = t_fin
        for s in reads:
            if s.r.get(k, (None, 0))[1] < ev[1]:
                s.r[k] = ev
        for s in writes:
            if s.w.get(k, (None, 0))[1] < ev[1]:
                s.w[k] = ev
        self.ops[e].append((list(deps.values()), fn, ev[0], inc))

    def replay(self, e, eng, final_waits=False):
        for waits, fn, sem, inc in self.ops[e]:
            for s, v in waits:
                eng.wait_ge(s, v)
            ins = fn(eng)
            if isinstance(ins, (list, tuple)):
                for i in ins:
                    i.then_inc(sem, inc)
            else:
                ins.then_inc(sem, inc)
        if final_waits:
            for d in self.dsems:
                for h, cnt in d.sub.values():
                    if cnt > 0:
                        eng.wait_ge(h, cnt)


class StopBuild(Exception):
    pass


KSTOP = [99]


def ckpt(n):
    if KSTOP[0] <= n:
        raise StopBuild()


class Ring:
    def __init__(self, items):
        self.items = items
        self.i = 0

    def next(self):
        it = self.items[self.i % len(self.items)]
        self.i += 1
        return it


def _stage1(N1sub, scale):
    nb = 128 // N1sub
    n = np.arange(N1sub)
    ang = 2 * np.pi * np.outer(n, n) / N1sub
    C = np.zeros((128, 128))
    S = np.zeros((128, 128))
    for b in range(nb):
        sl = slice(b * N1sub, (b + 1) * N1sub)
        C[sl, sl] = np.cos(ang) * scale
        S[sl, sl] = np.sin(ang) * scale
    return C, S


def _stage2(S_unit, pair):
    N2 = S_unit // 128
    G = 128 // N2
    out = np.zeros((N2, 128, 256))
    j = np.arange(G)[:, None, None]
    n2 = np.arange(N2)[None, :, None]
    m = np.arange(128)[None, None, :]
    for a in range(N2):
        if not pair:
            sel = (j == (m % G))
            k = a + N2 * m
            ang = 2 * np.pi * n2 * k / S_unit
        else:
            H = G // 2
            sel = ((j // H) == (m // 64)) & ((j % H) == ((m % 64) % H))
            k = a + N2 * (m % 64)
            ang = 2 * np.pi * n2 * k / (S_unit // 2)
        A = np.where(sel, np.cos(ang), 0.0).reshape(128, 128)
        B = np.where(sel, -np.sin(ang), 0.0).reshape(128, 128)
        out[a, :, :128] = A
        out[a, :, 128:] = B
    return out


def make_tables(S1, S2, pair):
    NB = (S1 + S2) // 128
    tb = np.zeros((128, TB16_W), np.float64)
    tb[:, T_ID:T_ID + 128] = np.eye(128)
    c = np.arange(128)
    ang = 2 * np.pi * np.outer(c, c) / 128
    tb[:, T_CS:T_CS + 128] = np.cos(ang)
    tb[:, T_CS + 128:T_CS + 256] = np.sin(ang)
    seq1 = S1 // 2 if pair else S1
    C, S = _stage1(64 if pair else 128, 1.0 / np.sqrt(seq1 * 128.0))
    tb[:, T_F1:T_F1 + 128] = C
    tb[:, T_F1 + 128:T_F1 + 256] = S
    tb[:, T_F1 + 256:T_F1 + 384] = -S
    C, S = _stage1(128, 1.0 / np.sqrt(S2 * 128.0))
    tb[:, T_F1 + 384:T_F1 + 512] = C
    tb[:, T_F1 + 512:T_F1 + 640] = S
    tb[:, T_F1 + 640:T_F1 + 768] = -S
    kj = np.arange(128)[:, None]
    qi = np.arange(128)[None, :]
    mL = (kj >= qi).astype(np.float64)
    mR = (kj <= qi).astype(np.float64)
    tb[:, T_MASK:T_MASK + 128] = mL
    tb[:, T_MASK + 128:T_MASK + 256] = mR
    if not pair:
        tb[:, T_MASK + 256:T_MASK + 384] = mL
        tb[:, T_MASK + 384:T_MASK + 512] = mR
    t2 = np.concatenate([_stage2(S1, pair), _stage2(S2, False)], axis=0)
    pos = np.concatenate([
        (np.arange(S1) % seq1), np.arange(S2)]).astype(np.float32)
    inv = (np.float32(500000.0) ** (-np.arange(0, 16, 2, dtype=np.float32) / np.float32(16))).astype(np.float32)
    a = pos[:, None] * inv[None, :]
    rope = np.concatenate([np.cos(a), np.sin(a)], axis=1).astype(np.float32)
    rope = rope.reshape(NB, 128, 16)
    return tb.astype(BF), t2.astype(BF), np.ascontiguousarray(rope)


def build_program(S1, S2, DEPTH):
    NT = S1 + S2
    NB = NT // 128
    NB1 = S1 // 128
    units = [(0, S1), (S1, S2)]
    nc = bass.Bass("TRN2", target_bir_lowering=False)

    def din(name, shape, dt):
        return nc.dram_tensor(name, list(shape), dt, kind="ExternalInput").ap()

    xin = din("xin", [NT, D], F32)
    pin = din("pin", [DEPTH, NT, PLE], F32)
    w_in = din("w_in", [DEPTH, D, IN_W], F32)
    w_fmix = din("w_fmix", [DEPTH, 512, 128], F32)
    w_ao = din("w_ao", [DEPTH, 512, D], F32)
    w_fo = din("w_fo", [DEPTH, 512, D], F32)
    w_out = din("w_out", [DEPTH, D, D], F32)
    w_pe = din("w_pe", [DEPTH, PLE, D], F32)
    w_pg = din("w_pg", [DEPTH, D, D], F32)
    smallp = din("smallp", [128, DEPTH * 24], F32)
    lnfd = din("lnfb", [128, D], F32)
    tb16 = din("tb16", [128, TB16_W], BF16)
    t2d = din("t2", [NB, 128, 256], BF16)
    roped = din("rope", [NB, 128, 16], F32)
    yout = nc.dram_tensor("yout", [NT, D], F32, kind="ExternalOutput").ap()
    hbuf = nc.dram_tensor("hbuf", [NT, D], F32, kind="Internal").ap()
    pqs = nc.dram_tensor("pqs", [NT, D], BF16, kind="Internal").ap()
    ysc = nc.dram_tensor("ysc", [NT, D], BF16, kind="Internal").ap()
    fsc = nc.dram_tensor("fsc", [NT, 512], BF16, kind="Internal").ap()

    es = ExitStack()
    with es:
        P = Prog(nc, es)

        def sb(name, shape, dt):
            return es.enter_context(nc.sbuf_tensor(name, list(shape), dt))

        def ps(name, shape, dt):
            return es.enter_context(nc.psum_tensor(name, list(shape), dt))

        tb = sb("tb_sb", [128, TB16_W], BF16)
        s_tb = Slot("tb")
        ident = tb[:, T_ID:T_ID + 128]
        csT = tb[:, T_CS:T_CS + 256]
        smp = sb("smp_sb", [128, DEPTH * 24], F32)
        s_smp = Slot("smp")
        esink = sb("esink", [128, DEPTH * 8], F32)
        s_esink = Slot("esink")
        stats = sb("stats", [128, NB], F32)
        rstd1 = sb("rstd1", [128, NB], F32)
        rstd1h = sb("rstd1h", [128, NB], F32)
        s_stats = [Slot("stats%d" % b) for b in range(NB)]
        s_rstd1 = Slot("rstd1")
        d_const = P.dsem()

        P.op("sp", lambda e: [e.dma_start(out=tb[:], in_=tb16),
                              e.dma_start(out=smp[:], in_=smallp)],
             writes=[s_tb, s_smp], dsem=d_const, ndma=2)
        for l in range(DEPTH):
            P.op("act", lambda e, l=l: e.activation(out=esink[:, l * 8:(l + 1) * 8],
                                                     in_=smp[:, l * 24 + 16:l * 24 + 24], func=AF.Exp),
                 reads=[s_smp], writes=[s_esink])

        wb_in = sb("wb_in", [128, 8, IN_W], BF16)
        wb_ao = sb("wb_ao", [128, 4, D], BF16)
        wb_fo = sb("wb_fo", [128, 4, D], BF16)
        wb_out = sb("wb_out", [128, 8, D], BF16)
        wb_pg = sb("wb_pg", [128, 8, D], BF16)
        wb_pe = sb("wb_pe", [128, 2, D], BF16)
        csw = sb("csw", [128, 4, 256], BF16)
        s_csw = Slot("csw")
        s_w = Slot("weights")
        conv_eng = Ring(["dve", "act"])

        s_w1 = Slot("weights_uf")
        f2_dsems = []

        def weight_jobs(l, part):
            jobs = []
            if part == 1:
                for kc in range(8):
                    rows = slice(kc * 128, (kc + 1) * 128)
                    jobs.append((w_in[l, rows, C_UF:C_UF + 512], wb_in[:, kc, C_UF:C_UF + 512], 512, False, s_w1))
                for kc in range(4):
                    rows = slice(kc * 128, (kc + 1) * 128)
                    jobs.append((w_fmix[l, rows, :], junk[:, kc * 128:(kc + 1) * 128], 128, False, s_junk))
                return jobs
            for kc in range(8):
                rows = slice(kc * 128, (kc + 1) * 128)
                for c0 in list(range(0, C_UF, 512)) + list(range(C_UF + 512, IN_W, 512)):
                    c1 = min(c0 + 512, C_UF if c0 < C_UF else IN_W)
                    jobs.append((w_in[l, rows, c0:c1], wb_in[:, kc, c0:c1], c1 - c0, c0 == 0, s_w))
            for wd, wb_, nk in ((w_ao, wb_ao, 4), (w_fo, wb_fo, 4), (w_out, wb_out, 8), (w_pg, wb_pg, 8), (w_pe, wb_pe, 2)):
                for kc in range(nk):
                    rows = slice(kc * 128, (kc + 1) * 128)
                    for c0 in (0, 512):
                        jobs.append((wd[l, rows, c0:c0 + 512], wb_[:, kc, c0:c0 + 512], 512, False, s_w))
            return jobs

        def weight_job(job, stg):
            src, dst, n, permq, s_dst = job
            st, s_st, d_st = stg
            P.op("sp", lambda e: e.dma_start(out=st[:, 0:n], in_=src), writes=[s_st], dsem=d_st, cost=2.0)
            ce = conv_eng.next()

            def conv(e):
                cp = e.tensor_copy if ce == "dve" else e.copy
                if permq:
                    return cp(out=dst[:, 0:512].rearrange("p (i j d) -> p i j d", i=4, j=2),
                              in_=st[:, 0:512].rearrange("p (j i d) -> p i j d", i=4, j=2))
                return cp(out=dst, in_=st[:, 0:n])
            P.op(ce, conv, reads=[s_st], writes=[s_dst], n=n)

        def weights_part2_gen(l):
            if not f2_dsems:
                f2_dsems.extend(P.dsem() for _ in r_f2.items)
            stgs = Ring([(t_, s_, d_) for (t_, s_), d_ in zip(r_f2.items, f2_dsems)])
            for job in weight_jobs(l, 2):
                weight_job(job, stgs.next())
                yield

        def load_weights(l):
            for job in weight_jobs(l, 1):
                weight_job(job, r_h.next())
            for gp in range(2):
                pm, s_pm = r_pm.next()

                def ffold(e, pm=pm, gp=gp):
                    for gg in range(2):
                        g = gp * 2 + gg
                        for q in range(2):
                            i = e.matmul(pm[:, gg * 256 + q * 128:gg * 256 + (q + 1) * 128],
                                         lhsT=tb[:, T_CS + q * 128:T_CS + (q + 1) * 128],
                                         rhs=junk[:, g * 128:(g + 1) * 128], start=True, stop=True)
                    return i
                P.op("pe", ffold, reads=[s_tb, s_junk], writes=[s_pm])
                P.op("dve", lambda e, pm=pm, gp=gp: e.tensor_copy(
                    out=csw[:, gp * 2:gp * 2 + 2, :], in_=pm[:].rearrange("p (g c) -> p g c", g=2)),
                    reads=[s_pm], writes=[s_csw])

        def ring(prefix, n, shape, dt, dma=False, halves=False):
            items = []
            for i in range(n):
                t = sb("%s%d" % (prefix, i), shape, dt)
                sl = [Slot(prefix + "_lo"), Slot(prefix + "_hi")] if halves else Slot(prefix)
                if dma:
                    items.append((t, sl, P.dsem()))
                else:
                    items.append((t, sl))
            return Ring(items)

        r_h = ring("hslot", 4, [128, D], F32, dma=True, halves=True)
        r_hb = ring("hbs", 1, [128, D], BF16)
        r_hT = ring("hTs", 3, [128, D], BF16)
        r_fi = ring("fis", 3, [128, 512], BF16, dma=True)
        r_p = ring("pslot", 4, [128, PLE], F32, dma=True)
        r_rope = ring("ropes", 2, [128, 16], F32, dma=True)
        r_qT = ring("qT", 3, [128, 4, 128], BF16)
        kT = [(sb("kT%d" % i, [128, 128], BF16), Slot("kT")) for i in range(4)]
        vA = [(sb("vA%d" % i, [128, 2, 65], BF16), Slot("vA")) for i in range(4)]
        qb_t = sb("qb_t", [128, 640], BF16)
        s_qbq, s_qbk = Slot("qbq"), Slot("qbk")
        qr_t = sb("qr_t", [128, 10, 16], F32)
        s_qr = Slot("qr")
        ropeA = sb("ropeA", [128, 10, 16], F32)
        ropeT = sb("ropeT", [128, 10, 16], F32)
        s_ropeA, s_ropeT = Slot("ropeA"), Slot("ropeT")
        pT_t = sb("pT_t", [128, 3072], BF16)
        s_pTs = [[Slot("pT%d%d" % (k, g)) for g in range(2)] for k in range(3)]
        r_f2 = ring("f2k", 3, [128, 512], F32)
        r_b1 = ring("b1k", 5, [128, 512], BF16, dma=True)
        r_m = ring("mgs", 2, [128, D], BF16, dma=True, halves=True)
        r_h1b = ring("h1bs", 2, [128, D], BF16, dma=True, halves=True)
        r_h1T = ring("h1Ts", 1, [128, D], BF16, dma=True)
        r_b2 = Ring(r_m.items + r_h1b.items + r_h1T.items)
        r_tg = ring("tgs", 1, [128, 512], F32)
        r_pb = ring("pbs", 2, [128, 256], BF16)
        r_big = ring("big", 2, [128, 2048], BF16, dma=True)
        r_t2 = ring("t2s", 2, [128, 256], BF16, dma=True)
        r_den = ring("den", 2, [128, 16], F32)
        r_sm = ring("sm", 2, [128, 16], F32)
        junk = sb("junk", [128, D], BF16)
        two_k = list(r_b2.items) + [(pT_t[:, c * 1024:(c + 1) * 1024], s_pTs[c], P.dsem()) for c in range(3)]
        hb_items = list(r_hb.items) + list(r_hT.items)
        s_junk = Slot("junk")

        r_pm = Ring([(ps("pm%d" % i, [128, 512], F32), Slot("pm", True)) for i in range(7)])
        r_pt = Ring([(ps("pt%d" % i, [128, 8, 128], BF16), Slot("pt", True)) for i in range(1)])

        for i in range(4):
            P.op("dve", lambda e, i=i: e.memset(vA[i][0][:, :, 64:65], 1.0), writes=[vA[i][1]])

        s_hb = [Slot("hb%d" % b) for b in range(NB)]
        s_pq = [Slot("pq%d" % u) for u in range(2)]
        s_ys = [Slot("ys%d" % u) for u in range(2)]
        s_fs = [Slot("fs%d" % u) for u in range(2)]

        evac_eng = Ring(["act", "dve"])

        def copy_on(eng_name):
            if eng_name == "dve":
                return lambda e, o, i: e.tensor_copy(out=o, in_=i)
            return lambda e, o, i: e.copy(out=o, in_=i)

        def transpose_to(src_ap, s_src, n, dst_ap, s_dst, gcol=None, extra_reads=(), eng=None):
            pt, s_pt = r_pt.next()

            def f(e):
                for k in range(n):
                    i = e.transpose(out=pt[:, k, :], in_=src_ap[:, k * 128:(k + 1) * 128], identity=ident)
                return i
            P.op("pe", f, reads=[s_src, s_tb], writes=[s_pt], cost=0.13 * n)
            if gcol is not None:
                P.op("dve", lambda e: e.tensor_tensor(
                    out=dst_ap, in0=pt[:, 0:n, :], in1=gcol.unsqueeze(2).to_broadcast([128, n, 128]), op=ALU.mult),
                    reads=[s_pt, s_smp] + list(extra_reads), writes=[s_dst], n=128 * n)
            else:
                en = eng or evac_eng.next()
                P.op(en, lambda e: copy_on(en)(e, dst_ap, pt[:, 0:n, :]), reads=[s_pt] + list(extra_reads), writes=[s_dst],
                     n=128 * n)

        def batch_rstd():
            P.op("dve", lambda e: e.tensor_scalar(out=rstd1[:], in0=stats[:], scalar1=1.0 / D, scalar2=EPS,
                                                   op0=ALU.mult, op1=ALU.add),
                 reads=s_stats, writes=[s_rstd1])
            P.op("act", lambda e: e.activation(out=rstd1[:], in_=rstd1[:], func=AF.Sqrt), reads=[s_rstd1], writes=[s_rstd1])
            P.op("dve", lambda e: e.reciprocal(out=rstd1[:], in_=rstd1[:]), reads=[s_rstd1], writes=[s_rstd1])
            P.op("dve", lambda e: e.tensor_scalar(out=rstd1h[:], in0=rstd1[:], scalar1=0.5, scalar2=None, op0=ALU.mult),
                 reads=[s_rstd1], writes=[s_rstd1])

        def src_rows(l, b):
            t = xin if l == 0 else hbuf
            return t[b * 128:(b + 1) * 128, :]

        for b in range(NB):
            ht, s_ht, d_ht = r_h.next()
            P.op("sp", lambda e, ht=ht, b=b: e.dma_start(out=ht[:], in_=src_rows(0, b)), writes=[s_ht], dsem=d_ht)
            P.op("act", lambda e, ht=ht, b=b: e.activation(out=junk[:], in_=ht[:], func=AF.Square,
                                                            accum_out=stats[:, b:b + 1]),
                 reads=[s_ht], writes=[s_junk, s_stats[b]])

        def do_layer(l):
            ckpt(1)
            load_weights(l)
            batch_rstd()
            ckpt(2)
            g1 = smp[:, l * 24:l * 24 + 8]
            gpg = smp[:, l * 24 + 8:l * 24 + 16]
            es_l = esink[:, l * 8:(l + 1) * 8]
            last = (l == DEPTH - 1)

            def do_unit(u, T0, SU):
                B0 = T0 // 128
                nbu = SU // 128
                N2 = SU // 128
                G = 128 // N2
                f1 = T_F1 + 384 * u
                C1 = tb[:, f1:f1 + 128]
                S1m = tb[:, f1 + 128:f1 + 256]
                nS1 = tb[:, f1 + 256:f1 + 384]

                SB_ = 2
                TW = SB_ * 128

                def schedule(threads):
                    threads = list(threads)
                    while threads:
                        progressed = False
                        for t in sorted(threads, key=lambda t: (t[1], t[2])):
                            P.step_fin = 0.0
                            try:
                                r = next(t[0])
                            except StopIteration:
                                threads.remove(t)
                                progressed = True
                                break
                            if r == "blocked":
                                continue
                            if P.step_fin > 0.0:
                                t[1] = P.step_fin
                            progressed = True
                            break
                        assert progressed, "schedule deadlock"

                def worker(queue, fn, w):
                    while queue:
                        item = queue.pop(0)
                        yield from fn(item, w)

                p1_rings = [dict(h=Ring(r_h.items[2 * w:2 * w + 2]), hb=Ring(hb_items[2 * w:2 * w + 2]),
                                 big=r_big.items[w], k2=Ring(two_k[4 * w:4 * w + 4])) for w in range(2)]

                def p1_supertile(st4, w):
                    R = p1_rings[w]
                    hT, s_hT, _ = R["big"]
                    hTv = hT[:, 0:8 * TW].rearrange("p (k t) -> p k t", k=8)
                    for j in range(SB_):
                        b = B0 + st4 * SB_ + j
                        ht, s_ht, d_ht = R["h"].next()
                        P.op("sp", lambda e, ht=ht, b=b: e.dma_start(out=ht[:], in_=src_rows(l, b)),
                             reads=[s_hb[b]], writes=[s_ht], dsem=d_ht)
                        yield
                        hb, s_hbf = R["hb"].next()[:2]
                        P.op("act", lambda e, hb=hb, ht=ht: e.copy(out=hb[:], in_=ht[:]), reads=[s_ht], writes=[s_hbf], n=1024)
                        yield
                        transpose_to(hb, s_hbf, 8, hTv[:, :, j * 128:(j + 1) * 128], s_hT, gcol=g1)
                        yield
                    ufT, s_ufT, _ = R["k2"].next()
                    ufv = ufT[:, 0:4 * TW].rearrange("p (g t) -> p g t", g=4)
                    for gp in range(2):
                        pm, s_pm = r_pm.next()

                        def f(e, pm=pm, gp=gp, hTv=hTv):
                            for gg in range(2):
                                g = gp * 2 + gg
                                for kc in range(8):
                                    i = e.matmul(pm[:, gg * TW:(gg + 1) * TW],
                                                 lhsT=wb_in[:, kc, C_UF + g * 128:C_UF + (g + 1) * 128],
                                                 rhs=hTv[:, kc, :], start=(kc == 0), stop=(kc == 7))
                            return i
                        P.op("pe", f, reads=[s_hT, s_w1], writes=[s_pm], cost=3.0)
                        en = evac_eng.next()
                        P.op(en, lambda e, en=en, gp=gp, pm=pm, ufv=ufv: copy_on(en)(
                            e, ufv[:, gp * 2:gp * 2 + 2, :], pm[:, 0:2 * TW].rearrange("p (g t) -> p g t", g=2)),
                            reads=[s_pm], writes=[s_ufT])
                        yield
                    for j in range(SB_):
                        b = B0 + st4 * SB_ + j
                        pqb, s_pqb, d_pqb = R["k2"].next()
                        pqv = pqb[:].rearrange("p (q g c) -> p q g c", q=2, g=4)
                        for half in range(2):
                            pm, s_pm = r_pm.next()

                            def f(e, pm=pm, half=half, j=j, ufv=ufv):
                                for gg in range(2):
                                    g = half * 2 + gg
                                    i = e.matmul(pm[:, gg * 256:(gg + 1) * 256],
                                                 lhsT=ufv[:, g, j * 128:(j + 1) * 128],
                                                 rhs=csw[:, g, :], start=True, stop=True)
                                return i
                            P.op("pe", f, reads=[s_ufT, s_csw], writes=[s_pm], cost=0.4)
                            P.op("act", lambda e, pm=pm, half=half, pqv=pqv, b=b: e.activation(
                                out=pqv[:, :, half * 2:half * 2 + 2, :].rearrange("p q g c -> p g q c"),
                                in_=pm[:].rearrange("p (g q c) -> p g q c", g=2, q=2),
                                func=AF.Copy, scale=rstd1[:, b:b + 1]),
                                reads=[s_pm, s_rstd1], writes=[s_pqb])
                        P.op("pool", lambda e, pqb=pqb, b=b: e.dma_start(out=pqs[b * 128:(b + 1) * 128, :], in_=pqb[:]),
                             reads=[s_pqb], writes=[s_pq[u]], dsem=d_pqb)
                        yield

                q1 = list(range(nbu // SB_))
                schedule([[worker(q1, p1_supertile, w), 0.0, w] for w in range(2)]
                         + ([[weights_part2_gen(l), 0.0, 5]] if u == 0 else []))

                ckpt(3)
                pq_v = pqs[T0:T0 + SU, :].rearrange("(a n) c -> a n c", n=N2)
                ys_v = ysc[T0:T0 + SU, :].rearrange("(a n) c -> a n c", n=N2)

                def f1_item(n2, w):
                    X, s_X, d_X = two_k[2 * w]
                    Y, s_Y, d_Y = two_k[2 * w + 1]
                    P.op("sp", lambda e: e.dma_start(out=X[:, 0:1024], in_=pq_v[:, n2, :]),
                         reads=[s_pq[u]], writes=[s_X], dsem=d_X)
                    yield
                    pmP, s_pmP = r_pm.next()
                    pmQ, s_pmQ = r_pm.next()

                    def f(e):
                        e.matmul(pmP[:], lhsT=C1, rhs=X[:, 0:512], start=True, stop=False)
                        e.matmul(pmP[:], lhsT=nS1, rhs=X[:, 512:1024], start=False, stop=True)
                        e.matmul(pmQ[:], lhsT=C1, rhs=X[:, 512:1024], start=True, stop=False)
                        return e.matmul(pmQ[:], lhsT=S1m, rhs=X[:, 0:512], start=False, stop=True)
                    P.op("pe", f, reads=[s_X, s_tb], writes=[s_pmP, s_pmQ], cost=1.6)
                    P.op("act", lambda e: e.copy(out=Y[:, 0:512], in_=pmP[:]), reads=[s_pmP], writes=[s_Y])
                    P.op("dve", lambda e: e.tensor_copy(out=Y[:, 512:1024], in_=pmQ[:]), reads=[s_pmQ], writes=[s_Y])
                    P.op("pool", lambda e: e.dma_start(out=ys_v[:, n2, :], in_=Y[:, 0:1024]),
                         reads=[s_Y], writes=[s_ys[u]], dsem=d_Y)
                    yield

                qf1 = list(range(N2))
                schedule([[worker(qf1, f1_item, w), 0.0, w] for w in range(4)])

                ckpt(4)
                ysu = ysc[T0:T0 + SU, :].rearrange("(k n) c -> k n c", n=N2)
                fsu = fsc[T0:T0 + SU, :].rearrange("(m a) c -> a m c", a=N2)
                f2_rings = [dict(k2=Ring(two_k[4 * w:4 * w + 4]), fo=Ring(r_b1.items[2 * w:2 * w + 2]),
                                 t2=r_t2.items[w]) for w in range(2)]

                def f2_item(a, w):
                    R = f2_rings[w]
                    Yt, s_Yt, d_Yt = R["k2"].next()

                    def ld(e):
                        return [e.dma_start(out=Yt[j * N2:(j + 1) * N2, :], in_=ysu[a + N2 * j, :, :]) for j in range(G)]
                    P.op("sp", ld, reads=[s_ys[u]], writes=[s_Yt], dsem=d_Yt, ndma=G)
                    tt, s_tt, d_tt = R["t2"]
                    gi = (0 if u == 0 else NB1) + a
                    P.op("sp", lambda e: e.dma_start(out=tt[:], in_=t2d[gi, :, :]), writes=[s_tt], dsem=d_tt)
                    yield
                    pm, s_pm = r_pm.next()

                    def f(e):
                        e.matmul(pm[:], lhsT=tt[:, 0:128], rhs=Yt[:, 0:512], start=True, stop=False)
                        return e.matmul(pm[:], lhsT=tt[:, 128:256], rhs=Yt[:, 512:1024], start=False, stop=True)
                    P.op("pe", f, reads=[s_tt, s_Yt], writes=[s_pm], cost=0.8)
                    fo, s_fo, d_fo = R["fo"].next()
                    en = evac_eng.next()
                    P.op(en, lambda e: copy_on(en)(e, fo[:], pm[:]), reads=[s_pm], writes=[s_fo])
                    P.op("pool", lambda e: e.dma_start(out=fsu[a, :, :], in_=fo[:]),
                         reads=[s_fo], writes=[s_fs[u]], dsem=d_fo)
                    yield

                qf2 = list(range(N2))
                schedule([[worker(qf2, f2_item, w), 0.0, w] for w in range(2)])

                ckpt(5)
                st = {}
                mid = (u == 0 and NB1 % 2 == 0)

                def proj512(hTv, s_hT, c0):
                    pm, s_pm = r_pm.next()

                    def f(e):
                        for kc in range(8):
                            i = e.matmul(pm[:], lhsT=hTv[:, kc, :], rhs=wb_in[:, kc, c0:c0 + 512],
                                         start=(kc == 0), stop=(kc == 7))
                        return i
                    P.op("pe", f, reads=[s_hT, s_w], writes=[s_pm], cost=2.6)
                    return pm, s_pm

                def gen_A(bl):
                    b = B0 + bl
                    rs = rstd1[:, b:b + 1]
                    ht, s_ht, d_ht = r_h.next()
                    P.op("sp", lambda e: e.dma_start(out=ht[:], in_=src_rows(l, b)), reads=[s_hb[b]], writes=[s_ht], dsem=d_ht)
                    pslot, s_p, d_p = r_p.next()
                    P.op("sp", lambda e: e.dma_start(out=pslot[:], in_=pin[l, b * 128:(b + 1) * 128, :]), writes=[s_p], dsem=d_p)
                    fi, s_fi, d_fi = r_fi.next()
                    P.op("sp", lambda e: e.dma_start(out=fi[:], in_=fsc[b * 128:(b + 1) * 128, :]),
                         reads=[s_fs[u]], writes=[s_fi], dsem=d_fi)
                    rp, s_rope, d_rp = r_rope.next()
                    P.op("sp", lambda e: e.dma_start(out=rp[:], in_=roped[b, :, :]), writes=[s_rope], dsem=d_rp)
                    yield
                    hb, s_hbf = r_hb.next()
                    P.op("act", lambda e: e.copy(out=hb[:], in_=ht[:]), reads=[s_ht], writes=[s_hbf], n=1024)
                    yield
                    hT, s_hT = r_hT.next()
                    hTv = hT[:].rearrange("p (k t) -> p k t", k=8)
                    transpose_to(hb, s_hbf, 8, hTv, s_hT, gcol=g1)
                    yield
                    pmq, s_pmq = r_pm.next()
                    pmk, s_pmk = r_pm.next()

                    def f(e):
                        for kc in range(8):
                            e.matmul(pmq[:], lhsT=hTv[:, kc, :], rhs=wb_in[:, kc, 0:512], start=(kc == 0), stop=(kc == 7))
                        for kc in range(8):
                            i = e.matmul(pmk[:, 0:256], lhsT=hTv[:, kc, :], rhs=wb_in[:, kc, 512:768],
                                         start=(kc == 0), stop=(kc == 7))
                        return i
                    P.op("pe", f, reads=[s_hT, s_w], writes=[s_pmq, s_pmk], cost=4.0)
                    P.op("dve", lambda e: e.tensor_scalar(
                        out=qr_t[:, 0:8, :], in0=pmq[:].rearrange("p (h d) -> p h d", d=64)[:, :, 0:16],
                        scalar1=rs, scalar2=None, op0=ALU.mult), reads=[s_pmq, s_rstd1], writes=[s_qr], n=128)
                    P.op("dve", lambda e: e.tensor_scalar(
                        out=qr_t[:, 8:10, :], in0=pmk[:, 0:128].rearrange("p (h d) -> p h d", d=64)[:, :, 0:16],
                        scalar1=rs, scalar2=None, op0=ALU.mult), reads=[s_pmk, s_rstd1], writes=[s_qr], n=32)
                    P.op("act", lambda e: e.activation(out=qb_t[:, 0:512], in_=pmq[:], func=AF.Copy, scale=rs),
                         reads=[s_pmq, s_rstd1], writes=[s_qbq])
                    P.op("act", lambda e: e.activation(out=qb_t[:, 512:640], in_=pmk[:, 0:128], func=AF.Copy, scale=rs),
                         reads=[s_pmk, s_rstd1], writes=[s_qbk], n=128)
                    va, s_va = vA[bl % 4]
                    P.op("act", lambda e: e.activation(out=va[:, :, 0:64], in_=pmk[:, 128:256].rearrange("p (g d) -> p g d", g=2),
                                                       func=AF.Copy, scale=rs),
                         reads=[s_pmk, s_rstd1], writes=[s_va], n=128)
                    yield
                    qr4 = qr_t[:].rearrange("p h (t d) -> p h t d", t=2)
                    rA4 = ropeA[:].rearrange("p h (t d) -> p h t d", t=2)
                    rT4 = ropeT[:].rearrange("p h (t d) -> p h t d", t=2)
                    cosb = rp[:, 0:8].unsqueeze(1).unsqueeze(1).to_broadcast([128, 10, 2, 8])
                    sinb = rp[:, 8:16].unsqueeze(1).unsqueeze(1).to_broadcast([128, 10, 2, 8])
                    P.op("dve", lambda e: e.tensor_tensor(out=rA4, in0=qr4, in1=cosb, op=ALU.mult),
                         reads=[s_qr, s_rope], writes=[s_ropeA], n=160)
                    P.op("dve", lambda e: e.tensor_tensor(out=rT4, in0=qr4, in1=sinb, op=ALU.mult),
                         reads=[s_qr, s_rope], writes=[s_ropeT], n=160)
                    qbv = qb_t[:].rearrange("p (h d) -> p h d", d=64)
                    P.op("dve", lambda e: e.tensor_tensor(out=qbv[:, :, 0:8], in0=rA4[:, :, 0, :], in1=rT4[:, :, 1, :], op=ALU.subtract),
                         reads=[s_ropeA, s_ropeT], writes=[s_qbq, s_qbk], n=80)
                    P.op("dve", lambda e: e.tensor_tensor(out=qbv[:, :, 8:16], in0=rA4[:, :, 1, :], in1=rT4[:, :, 0, :], op=ALU.add),
                         reads=[s_ropeA, s_ropeT], writes=[s_qbq, s_qbk], n=80)
                    yield
                    qT, s_qT = r_qT.next()
                    kt, s_kt = kT[bl % 4]
                    pt, s_pt = r_pt.next()

                    def ft(e):
                        for k in range(5):
                            i = e.transpose(out=pt[:, k, :], in_=qb_t[:, k * 128:(k + 1) * 128], identity=ident)
                        return i
                    P.op("pe", ft, reads=[s_qbq, s_qbk, s_tb], writes=[s_pt], cost=0.65)
                    P.op("act", lambda e: e.copy(out=qT[:], in_=pt[:, 0:4, :]), reads=[s_pt], writes=[s_qT])
                    P.op("act", lambda e: e.copy(out=kt[:], in_=pt[:, 4, :]), reads=[s_pt], writes=[s_kt], n=128)
                    st[bl] = dict(ht=ht, s_ht=s_ht, d_ht=d_ht, hTv=hTv, s_hT=s_hT, qT=qT, s_qT=s_qT,
                                  pslot=pslot, s_p=s_p, fi=fi, s_fi=s_fi)

                def gen_B1(bl):
                    b = B0 + bl
                    S_ = st[bl]
                    ht, s_ht, hTv, s_hT = S_["ht"], S_["s_ht"], S_["hTv"], S_["s_hT"]
                    qT, s_qT = S_["qT"], S_["s_qT"]
                    rs = rstd1[:, b:b + 1]
                    rsh = rstd1h[:, b:b + 1]
                    kbs = [kb for kb in (-1, 0, 1) if 0 <= bl + kb < nbu]
                    pTv = pT_t[:].rearrange("p (k g c) -> p k g c", k=3, g=2)
                    for kb in kbs:
                        kt, s_kt = kT[(bl + kb) % 4]
                        for g in range(2):
                            pm, s_pm = r_pm.next()
                            P.op("pe", lambda e, pm=pm, kt=kt, g=g: e.matmul(
                                pm[:], lhsT=kt[64 * g:64 * g + 64, :], rhs=qT[64 * g:64 * g + 64, :, :].rearrange("p i t -> p (i t)"),
                                start=True, stop=True), reads=[s_kt, s_qT], writes=[s_pm], cost=0.3)
                            P.op("act", lambda e, pm=pm, kb=kb, g=g: e.activation(
                                out=pTv[:, kb + 1, g, :], in_=pm[:], func=AF.Exp, scale=0.125),
                                reads=[s_pm], writes=[s_pTs[kb + 1][g]])
                        if kb != 0:
                            mo = T_MASK + (0 if kb < 0 else 128)
                            if mid and ((kb < 0 and bl == NB1 // 2) or (kb > 0 and bl == NB1 // 2 - 1)):
                                mo += 256
                            mk = tb[:, mo:mo + 128].unsqueeze(1).to_broadcast([128, 8, 128])
                            P.op("dve", lambda e, kb=kb, mk=mk: e.tensor_tensor(
                                out=pTv[:, kb + 1, :, :].rearrange("p g (i t) -> p (g i) t", i=4),
                                in0=pTv[:, kb + 1, :, :].rearrange("p g (i t) -> p (g i) t", i=4), in1=mk, op=ALU.mult),
                                reads=[s_tb] + s_pTs[kb + 1], writes=s_pTs[kb + 1], n=1024)
                    yield
                    pga, s_pga = proj512(hTv, s_hT, C_GA)
                    sg, s_sg = r_f2.next()
                    P.op("act", lambda e: e.activation(out=sg[:], in_=pga[:], func=AF.Tanh, scale=rsh),
                         reads=[s_pga, s_rstd1], writes=[s_sg])
                    P.op("dve", lambda e: e.scalar_tensor_tensor(out=sg[:], in0=sg[:], scalar=1.0, in1=pga[:],
                                                                 op0=ALU.add, op1=ALU.mult),
                         reads=[s_sg, s_pga], writes=[s_sg])
                    yield
                    pgf, s_pgf = proj512(hTv, s_hT, C_GF)
                    sgf, s_sgf = r_f2.next()
                    P.op("act", lambda e: e.activation(out=sgf[:], in_=pgf[:], func=AF.Tanh, scale=rsh),
                         reads=[s_pgf, s_rstd1], writes=[s_sgf])
                    P.op("dve", lambda e: e.scalar_tensor_tensor(out=sgf[:], in0=sgf[:], scalar=1.0, in1=pgf[:],
                                                                 op0=ALU.add, op1=ALU.mult),
                         reads=[s_sgf, s_pgf], writes=[s_sgf])
                    yield
                    fi, s_fi = S_["fi"], S_["s_fi"]
                    yb, s_yb, _ = r_b1.next()
                    P.op("dve", lambda e: e.scalar_tensor_tensor(out=yb[:], in0=fi[:], scalar=rs, in1=sgf[:],
                                                                 op0=ALU.mult, op1=ALU.mult),
                         reads=[s_fi, s_sgf, s_rstd1], writes=[s_yb])
                    yield
                    yT, s_yT, _ = r_b1.next()
                    yTv = yT[:].rearrange("p (g t) -> p g t", g=4)
                    transpose_to(yb, s_yb, 4, yTv, s_yT)
                    yield
                    po = [r_pm.next(), r_pm.next()]

                    def fpv(e):
                        for h in range(8):
                            g, i_ = h // 4, h % 4
                            pm = po[h // 4][0]
                            for n_, kb in enumerate(kbs):
                                va, _ = vA[(bl + kb) % 4]
                                ins = e.matmul(pm[:, i_ * 72:i_ * 72 + 65], lhsT=pTv[:, kb + 1, g, i_ * 128:(i_ + 1) * 128],
                                               rhs=va[:, g, :], start=(n_ == 0), stop=(n_ == len(kbs) - 1))
                        return ins
                    P.op("pe", fpv, reads=[x for kb in kbs for x in s_pTs[kb + 1]] + [vA[(bl + kb) % 4][1] for kb in kbs],
                         writes=[po[0][1], po[1][1]], cost=1.6)
                    den, s_den = r_den.next()
                    ab, s_ab, _ = r_b1.next()
                    at, s_at = r_f2.next()
                    for hh in range(2):
                        pm, s_pm = po[hh]
                        pv = pm[:, 0:288].rearrange("p (i c) -> p i c", i=4)
                        dsl = den[:, hh * 4:(hh + 1) * 4]
                        P.op("dve", lambda e, pv=pv, dsl=dsl, hh=hh: e.tensor_tensor(
                            out=dsl.unsqueeze(2), in0=pv[:, :, 64:65], in1=es_l[:, hh * 4:(hh + 1) * 4].unsqueeze(2), op=ALU.add),
                            reads=[s_pm, s_esink], writes=[s_den], n=4)
                        P.op("dve", lambda e, dsl=dsl: e.reciprocal(out=dsl, in_=dsl), reads=[s_den], writes=[s_den], n=4)
                        P.op("dve", lambda e, pv=pv, dsl=dsl, hh=hh: e.tensor_tensor(
                            out=at[:, hh * 256:(hh + 1) * 256].rearrange("p (i d) -> p i d", i=4),
                            in0=pv[:, :, 0:64], in1=dsl.unsqueeze(2).to_broadcast([128, 4, 64]), op=ALU.mult),
                            reads=[s_pm, s_den], writes=[s_at], n=256)
                    P.op("dve", lambda e: e.scalar_tensor_tensor(out=ab[:], in0=at[:], scalar=rs, in1=sg[:],
                                                                 op0=ALU.mult, op1=ALU.mult),
                         reads=[s_at, s_sg, s_rstd1], writes=[s_ab])
                    yield
                    mg, s_mg, _ = r_m.next()
                    aTv = None
                    for half in range(2):
                        cs_ = slice(half * 512, (half + 1) * 512)
                        pma, s_pma = proj512(hTv, s_hT, C_MGA + half * 512)
                        tma, s_tma = r_f2.next()
                        P.op("act", lambda e, tma=tma, pma=pma: e.activation(out=tma[:], in_=pma[:], func=AF.Tanh, scale=rsh),
                             reads=[s_pma, s_rstd1], writes=[s_tma])
                        yield
                        pmf, s_pmf = proj512(hTv, s_hT, C_MGF + half * 512)
                        tmf, s_tmf = r_f2.next()
                        P.op("act", lambda e, tmf=tmf, pmf=pmf: e.activation(out=tmf[:], in_=pmf[:], func=AF.Tanh, scale=rsh),
                             reads=[s_pmf, s_rstd1], writes=[s_tmf])
                        yield
                        if aTv is None:
                            aT, s_aT, _ = r_b1.next()
                            aTv = aT[:].rearrange("p (g t) -> p g t", g=4)
                            transpose_to(ab, s_ab, 4, aTv, s_aT)
                            yield
                        pA, s_pA = r_pm.next()
                        pF, s_pF = r_pm.next()

                        def fa(e, pA=pA, cs_=cs_):
                            for kc in range(4):
                                i = e.matmul(pA[:], lhsT=aTv[:, kc, :], rhs=wb_ao[:, kc, cs_], start=(kc == 0), stop=(kc == 3))
                            return i
                        P.op("pe", fa, reads=[s_aT, s_w], writes=[s_pA], cost=1.3)

                        def ff(e, pF=pF, cs_=cs_):
                            for kc in range(4):
                                i = e.matmul(pF[:], lhsT=yTv[:, kc, :], rhs=wb_fo[:, kc, cs_], start=(kc == 0), stop=(kc == 3))
                            return i
                        P.op("pe", ff, reads=[s_yT, s_w], writes=[s_pF], cost=1.3)
                        P.op("dve", lambda e, tma=tma, pA=pA: e.scalar_tensor_tensor(
                            out=tma[:], in0=tma[:], scalar=1.0, in1=pA[:], op0=ALU.add, op1=ALU.mult),
                            reads=[s_tma, s_pA], writes=[s_tma])
                        P.op("dve", lambda e, tmf=tmf, pF=pF: e.scalar_tensor_tensor(
                            out=tmf[:], in0=tmf[:], scalar=1.0, in1=pF[:], op0=ALU.add, op1=ALU.mult),
                            reads=[s_tmf, s_pF], writes=[s_tmf])
                        P.op("dve", lambda e, tma=tma, tmf=tmf, cs_=cs_: e.tensor_tensor(
                            out=mg[:, cs_], in0=tma[:], in1=tmf[:], op=ALU.add), reads=[s_tma, s_tmf], writes=[s_mg[half]])
                        yield
                    mT, s_mT, _ = r_m.next()
                    mTv = mT[:].rearrange("p (k t) -> p k t", k=8)
                    transpose_to(mg, s_mg, 8, mTv, s_mT)
                    yield
                    h1b, s_h1b, _ = r_h1b.next()
                    pOs = []
                    for half in range(2):
                        cs_ = slice(half * 512, (half + 1) * 512)
                        pO, s_pO = r_pm.next()

                        def fo_(e, pO=pO, cs_=cs_):
                            for kc in range(8):
                                i = e.matmul(pO[:], lhsT=mTv[:, kc, :], rhs=wb_out[:, kc, cs_], start=(kc == 0), stop=(kc == 7))
                            return i
                        P.op("pe", fo_, reads=[s_mT, s_w], writes=[s_pO], cost=2.6)
                        P.op("dve", lambda e, pO=pO, cs_=cs_: e.scalar_tensor_tensor(
                            out=h1b[:, cs_], in0=pO[:], scalar=0.25, in1=ht[:, cs_], op0=ALU.mult, op1=ALU.add),
                            reads=[s_pO, s_ht[half]], writes=[s_h1b[half]])
                        pOs.append((pO, s_pO, cs_, half))
                    for pO, s_pO, cs_, half in pOs:
                        P.op("dve", lambda e, pO=pO, cs_=cs_: e.scalar_tensor_tensor(
                            out=ht[:, cs_], in0=pO[:], scalar=0.25, in1=ht[:, cs_], op0=ALU.mult, op1=ALU.add),
                            reads=[s_pO, s_ht[half]], writes=[s_ht[half]])
                    S_["h1b"], S_["s_h1b"] = h1b, s_h1b

                def gen_B2(bl):
                    b = B0 + bl
                    S_ = st.pop(bl)
                    ht, s_ht, d_ht = S_["ht"], S_["s_ht"], S_["d_ht"]
                    h1b, s_h1b = S_["h1b"], S_["s_h1b"]
                    pslot, s_p = S_["pslot"], S_["s_p"]
                    pb, s_pb = r_pb.next()
                    P.op("act", lambda e: e.copy(out=pb[:], in_=pslot[:]), reads=[s_p], writes=[s_pb], n=256)
                    sm, s_sm = r_sm.next()
                    P.op("act", lambda e: e.activation(out=junk[:], in_=ht[:], func=AF.Square, accum_out=sm[:, 0:1]),
                         reads=[s_ht], writes=[s_junk, s_sm], n=1024)
                    h1T, s_h1T, _ = r_h1T.next()
                    h1Tv = h1T[:].rearrange("p (k t) -> p k t", k=8)
                    transpose_to(h1b, s_h1b, 8, h1Tv, s_h1T, gcol=gpg)
                    yield
                    P.op("dve", lambda e: e.tensor_scalar(out=sm[:, 1:2], in0=sm[:, 0:1], scalar1=1.0 / D, scalar2=EPS,
                                                          op0=ALU.mult, op1=ALU.add), reads=[s_sm], writes=[s_sm], n=1)
                    P.op("act", lambda e: e.activation(out=sm[:, 2:3], in_=sm[:, 1:2], func=AF.Sqrt), reads=[s_sm], writes=[s_sm], cost=2.9)
                    P.op("dve", lambda e: e.reciprocal(out=sm[:, 3:4], in_=sm[:, 2:3]), reads=[s_sm], writes=[s_sm], n=1)
                    P.op("dve", lambda e: e.tensor_scalar(out=sm[:, 4:5], in0=sm[:, 3:4], scalar1=0.5, scalar2=None, op0=ALU.mult),
                         reads=[s_sm], writes=[s_sm], n=1)
                    yield
                    pTt, s_pTt = r_pb.next()
                    pTtv = pTt[:].rearrange("p (k t) -> p k t", k=2)
                    transpose_to(pb, s_pb, 2, pTtv, s_pTt)
                    yield
                    for half in range(2):
                        cs_ = slice(half * 512, (half + 1) * 512)
                        pG, s_pG = r_pm.next()

                        def fg(e, pG=pG, cs_=cs_):
                            for kc in range(8):
                                i = e.matmul(pG[:], lhsT=h1Tv[:, kc, :], rhs=wb_pg[:, kc, cs_], start=(kc == 0), stop=(kc == 7))
                            return i
                        P.op("pe", fg, reads=[s_h1T, s_w], writes=[s_pG], cost=2.6)
                        tg, s_tg = r_tg.next()
                        P.op("act", lambda e, tg=tg, pG=pG: e.activation(out=tg[:], in_=pG[:], func=AF.Tanh, scale=sm[:, 4:5]),
                             reads=[s_pG, s_sm], writes=[s_tg])
                        pE, s_pE = r_pm.next()

                        def fe(e, pE=pE, cs_=cs_):
                            for kc in range(2):
                                i = e.matmul(pE[:], lhsT=pTtv[:, kc, :], rhs=wb_pe[:, kc, cs_], start=(kc == 0), stop=(kc == 1))
                            return i
                        P.op("pe", fe, reads=[s_pTt, s_w], writes=[s_pE], cost=0.65)
                        P.op("dve", lambda e, tg=tg, pE=pE: e.scalar_tensor_tensor(
                            out=tg[:], in0=tg[:], scalar=1.0, in1=pE[:], op0=ALU.add, op1=ALU.mult),
                            reads=[s_tg, s_pE], writes=[s_tg])
                        P.op("dve", lambda e, tg=tg, cs_=cs_: e.scalar_tensor_tensor(
                            out=ht[:, cs_], in0=tg[:], scalar=0.5, in1=ht[:, cs_], op0=ALU.mult, op1=ALU.add),
                            reads=[s_tg, s_ht[half]], writes=[s_ht[half]])
                        yield
                    P.op("act", lambda e: e.activation(out=junk[:], in_=ht[:], func=AF.Square, accum_out=stats[:, b:b + 1]),
                         reads=[s_ht], writes=[s_junk, s_stats[b]], n=1024)
                    P.op("pool", lambda e: e.dma_start(out=hbuf[b * 128:(b + 1) * 128, :], in_=ht[:]),
                         reads=[s_ht], writes=[s_hb[b]], dsem=d_ht)

                return gen_A, gen_B1, gen_B2, nbu, schedule

            ug = [do_unit(u, T0, SU) for u, (T0, SU) in enumerate(units)]
            seq = [(u, bl) for u in range(len(units)) for bl in range(ug[u][3])]
            ntot = len(seq)
            prog = {"A": 0, "B1": 0, "B2": 0}

            def role(name, idx, can_start):
                for J, (u, bl) in enumerate(seq):
                    while not can_start(J):
                        yield "blocked"
                    yield from ug[u][idx](bl)
                    prog[name] = J + 1

            roles = [
                [role("B1", 1, lambda j: prog["A"] >= min(j + 2, ntot) and prog["B2"] >= j - 1), 0.0, 0],
                [role("A", 0, lambda j: prog["B2"] >= j - 3 and prog["B1"] >= j - 2), 0.0, 1],
                [role("B2", 2, lambda j: prog["B1"] >= j + 1), 0.0, 2],
            ]
            ug[0][4](roles)

        try:
            for l in range(DEPTH):
                do_layer(l)
        except StopBuild:
            pass

        batch_rstd()
        lnf_t, s_lnf, d_lnf = r_h.items[0]
        lnf_b = lnf_t[:]
        r_hf = Ring(r_h.items[1:])
        P.op("sp", lambda e: e.dma_start(out=lnf_b, in_=lnfd), writes=[s_lnf], dsem=d_lnf)
        for b in range(NB):
            ht, s_ht, d_ht = r_hf.next()
            P.op("sp", lambda e, ht=ht, b=b: e.dma_start(out=ht[:], in_=hbuf[b * 128:(b + 1) * 128, :]),
                 reads=[s_hb[b]], writes=[s_ht], dsem=d_ht)
            P.op("dve", lambda e, ht=ht, b=b: e.scalar_tensor_tensor(
                out=ht[:], in0=ht[:], scalar=rstd1[:, b:b + 1], in1=lnf_b, op0=ALU.mult, op1=ALU.mult),
                reads=[s_ht, s_rstd1, s_lnf], writes=[s_ht])
            P.op("pool", lambda e, ht=ht, b=b: e.dma_start(out=yout[b * 128:(b + 1) * 128, :], in_=ht[:]),
                 reads=[s_ht], dsem=d_ht)

        import sys as _sys
        print("[build] sbuf bytes remaining per partition:", nc.sbuf_bytes_remaining, file=_sys.stderr)
        with nc.Block() as block:
            @block.sync
            def _(eng):
                P.replay("sp", eng)

            @block.scalar
            def _(eng):
                P.replay("act", eng)

            @block.vector
            def _(eng):
                P.replay("dve", eng)

            @block.tensor
            def _(eng):
                P.replay("pe", eng)

            @block.gpsimd
            def _(eng):
                P.replay("pool", eng, final_waits=True)
    return nc


def pack_small(ln1, ln_pg, sink, ln_f, DEPTH):
    sp = np.zeros((128, DEPTH * 24), np.float32)
    for l in range(DEPTH):
        sp[:, l * 24:l * 24 + 8] = ln1[l].reshape(8, 128).T
        sp[:, l * 24 + 8:l * 24 + 16] = ln_pg[l].reshape(8, 128).T
        sp[:, l * 24 + 16:l * 24 + 24] = np.broadcast_to(sink[l][None, :], (128, 8))
    return sp, np.ascontiguousarray(np.broadcast_to(ln_f[None, :], (128, D))).astype(np.float32)


def run_cores(core_inputs, S1, S2, DEPTH, weights):
    nc = build_program(S1, S2, DEPTH)
    tabs = {}
    in_maps = []
    smallp, lnfb = pack_small(weights["ln1"], weights["ln_pg"], weights["sink"], weights["ln_f"], DEPTH)
    shared = dict(
        w_in=np.ascontiguousarray(weights["w_in"]),
        w_fmix=np.ascontiguousarray(weights["w_fmix"].reshape(DEPTH, 512, 128)),
        w_ao=np.ascontiguousarray(weights["w_ao"]), w_fo=np.ascontiguousarray(weights["w_fo"]),
        w_out=np.ascontiguousarray(weights["w_out"]), w_pe=np.ascontiguousarray(weights["w_pe"]),
        w_pg=np.ascontiguousarray(weights["w_pg"]), smallp=smallp, lnfb=lnfb)
    for ci in core_inputs:
        pair = ci["pair"]
        if pair not in tabs:
            tabs[pair] = make_tables(S1, S2, pair)
        tb, t2, rope = tabs[pair]
        m = dict(shared)
        m.update(xin=np.ascontiguousarray(ci["x"]), pin=np.ascontiguousarray(ci["p"]), tb16=tb, t2=t2, rope=rope)
        in_maps.append(m)
    res = run_bass_kernel_spmd(nc, in_maps, core_ids=list(range(len(in_maps))))
    return [np.asarray(r["yout"]) for r in res.results]


def kernel(x_prompt, x_sample, p_prompt, p_sample, ln1, w_in, sink, w_fmix, w_ao, w_fo,
           w_out, w_pe, ln_pg, w_pg, ln_f):
    f = lambda a: np.asarray(a, dtype=np.float32)
    x_prompt, x_sample, p_prompt, p_sample = f(x_prompt), f(x_sample), f(p_prompt), f(p_sample)
    weights = dict(ln1=f(ln1), w_in=f(w_in), sink=f(sink), w_fmix=f(w_fmix), w_ao=f(w_ao), w_fo=f(w_fo),
                   w_out=f(w_out), w_pe=f(w_pe), ln_pg=f(ln_pg), w_pg=f(w_pg), ln_f=f(ln_f))
    DEPTH = w_in.shape[0]
    S1, S2 = 8192, 4096
    cores = []
    assign = []
    for c in range(4):
        x = np.concatenate([x_sample[c], x_prompt[c]], axis=0)
        p = np.concatenate([p_sample[:, c], p_prompt[:, c]], axis=1)
        cores.append(dict(x=x, p=p, pair=False))
        assign.append((("s", c), ("p", c)))
    for c in range(4):
        i0 = 4 + 3 * c
        x = np.concatenate([x_prompt[i0], x_prompt[i0 + 1], x_prompt[i0 + 2]], axis=0)
        p = np.concatenate([p_prompt[:, i0], p_prompt[:, i0 + 1], p_prompt[:, i0 + 2]], axis=1)
        cores.append(dict(x=x, p=p, pair=True))
        assign.append((("p", i0), ("p", i0 + 1), ("p", i0 + 2)))
    outs = run_cores(cores, S1, S2, DEPTH, weights)
    y_prompt = np.empty_like(x_prompt)
    y_sample = np.empty_like(x_sample)
    for c in range(8):
        o = outs[c]
        if c < 4:
            y_sample[c] = o[0:8192]
            y_prompt[c] = o[8192:12288]
        else:
            i0 = 4 + 3 * (c - 4)
            y_prompt[i0] = o[0:4096]
            y_prompt[i0 + 1] = o[4096:8192]
            y_prompt[i0 + 2] = o[8192:12288]
    return (y_prompt, y_sample)
```

```python
import numpy as np
import ml_dtypes
from contextlib import ExitStack

import concourse.bass as bass
import concourse.mybir as mybir
from concourse.bass_utils import run_bass_kernel_spmd

F32 = mybir.dt.float32
BF16 = mybir.dt.bfloat16
AF = mybir.ActivationFunctionType
ALU = mybir.AluOpType
AX = mybir.AxisListType
BF = ml_dtypes.bfloat16

D = 1024
IN_W = 4352
PLE = 256
EPS = 1e-6
C_Q, C_K, C_V, C_GA, C_UF, C_GF, C_MGA, C_MGF = 0, 512, 640, 768, 1280, 1792, 2304, 3328

T_ID = 0
T_CS = 128
T_F1 = 384
T_MASK = T_F1 + 768
TB16_W = T_MASK + 512


class Slot:
    __slots__ = ("w", "r", "name", "excl")

    def __init__(self, name="", excl=False):
        self.w = {}
        self.r = {}
        self.name = name
        self.excl = excl


class DSem:
    __slots__ = ("prog", "sub")

    def __init__(self, prog):
        self.prog = prog
        self.sub = {}

    def get(self, e):
        if e not in self.sub:
            self.sub[e] = [self.prog.new_sem(), 0]
        return self.sub[e]


class Prog:
    ENGS = ("pe", "act", "dve", "pool", "sp")

    def __init__(self, nc, es):
        self.nc = nc
        self.es = es
        self.esem = {e: es.enter_context(nc.semaphore("es_" + e)) for e in self.ENGS}
        self.cnt = {e: 0 for e in self.ENGS}
        self.waited = {e: {} for e in self.ENGS}
        self.ops = {e: [] for e in self.ENGS}
        self.dsems = []
        self.nsem = 0
        self.free = {e: 0.0 for e in self.ENGS}
        self.fin = {}
        self.step_fin = 0.0

    def new_sem(self):
        self.nsem += 1
        return self.es.enter_context(self.nc.semaphore("ds%d" % self.nsem))

    def dsem(self):
        d = DSem(self)
        self.dsems.append(d)
        return d

    def op(self, e, fn, reads=(), writes=(), dsem=None, ndma=1, n=512, cost=None):
        deps = {}
        own = id(self.esem[e])
        reads = [x for s_ in reads for x in (s_ if isinstance(s_, (list, tuple)) else [s_])]
        writes = [x for s_ in writes for x in (s_ if isinstance(s_, (list, tuple)) else [s_])]

        def need(key, sem, val):
            if key == own and e == "pe" and dsem is None:
                return
            if self.waited[e].get(key, 0) >= val:
                return
            if key not in deps or deps[key][1] < val:
                deps[key] = (sem, val)

        for s in reads:
            for key, (sem, val) in s.w.items():
                need(key, sem, val)
            if s.excl:
                for key, (sem, val) in s.r.items():
                    if key != own:
                        need(key, sem, val)
        for s in writes:
            for key, (sem, val) in s.w.items():
                need(key, sem, val)
            for key, (sem, val) in s.r.items():
                need(key, sem, val)
        t_start = self.free[e]
        for key, (sem, val) in deps.items():
            self.waited[e][key] = val
            t_start = max(t_start, self.fin.get((key, val), 0.0) + 0.12)
        if cost is None:
            if dsem is not None:
                cost = 2.5
            elif e == "pe":
                cost = 1.0
            elif e == "act":
                cost = 0.22 + n / 1200.0
            else:
                cost = 0.15 + n / 900.0
        t_fin = t_start + cost
        self.free[e] = (t_start + 0.1 * ndma) if dsem is not None else t_fin
        self.step_fin = max(self.step_fin, t_fin)
        if dsem is None:
            self.cnt[e] += 1
            ev = (self.esem[e], self.cnt[e])
            inc = 1
        else:
            sub = dsem.get(e)
            sub[1] += 16 * ndma
            ev = (sub[0], sub[1])
            inc = 16
        k = id(ev[0])
        self.fin[(k, ev[1])] = t_fin
        for s in reads:
            if s.r.get(k, (None, 0))[1] < ev[1]:
                s.r[k] = ev
        for s in writes:
            if s.w.get(k, (None, 0))[1] < ev[1]:
                s.w[k] = ev
        self.ops[e].append((list(deps.values()), fn, ev[0], inc))

    def replay(self, e, eng, final_waits=False):
        for waits, fn, sem, inc in self.ops[e]:
            for s, v in waits:
                eng.wait_ge(s, v)
            ins = fn(eng)
            if isinstance(ins, (list, tuple)):
                for i in ins:
                    i.then_inc(sem, inc)
            else:
                ins.then_inc(sem, inc)
        if final_waits:
            for d in self.dsems:
                for h, cnt in d.sub.values():
                    if cnt > 0:
                        eng.wait_ge(h, cnt)


class StopBuild(Exception):
    pass


KSTOP = [99]


def ckpt(n):
    if KSTOP[0] <= n:
        raise StopBuild()


class Ring:
    def __init__(self, items):
        self.items = items
        self.i = 0

    def next(self):
        it = self.items[self.i % len(self.items)]
        self.i += 1
        return it


def _stage1(N1sub, scale):
    nb = 128 // N1sub
    n = np.arange(N1sub)
    ang = 2 * np.pi * np.outer(n, n) / N1sub
    C = np.zeros((128, 128))
    S = np.zeros((128, 128))
    for b in range(nb):
        sl = slice(b * N1sub, (b + 1) * N1sub)
        C[sl, sl] = np.cos(ang) * scale
        S[sl, sl] = np.sin(ang) * scale
    return C, S


def _stage2(S_unit, pair):
    N2 = S_unit // 128
    G = 128 // N2
    out = np.zeros((N2, 128, 256))
    j = np.arange(G)[:, None, None]
    n2 = np.arange(N2)[None, :, None]
    m = np.arange(128)[None, None, :]
    for a in range(N2):
        if not pair:
            sel = (j == (m % G))
            k = a + N2 * m
            ang = 2 * np.pi * n2 * k / S_unit
        else:
            H = G // 2
            sel = ((j // H) == (m // 64)) & ((j % H) == ((m % 64) % H))
            k = a + N2 * (m % 64)
            ang = 2 * np.pi * n2 * k / (S_unit // 2)
        A = np.where(sel, np.cos(ang), 0.0).reshape(128, 128)
        B = np.where(sel, -np.sin(ang), 0.0).reshape(128, 128)
        out[a, :, :128] = A
        out[a, :, 128:] = B
    return out


def make_tables(S1, S2, pair):
    NB = (S1 + S2) // 128
    tb = np.zeros((128, TB16_W), np.float64)
    tb[:, T_ID:T_ID + 128] = np.eye(128)
    c = np.arange(128)
    ang = 2 * np.pi * np.outer(c, c) / 128
    tb[:, T_CS:T_CS + 128] = np.cos(ang)
    tb[:, T_CS + 128:T_CS + 256] = np.sin(ang)
    seq1 = S1 // 2 if pair else S1
    C, S = _stage1(64 if pair else 128, 1.0 / np.sqrt(seq1 * 128.0))
    tb[:, T_F1:T_F1 + 128] = C
    tb[:, T_F1 + 128:T_F1 + 256] = S
    tb[:, T_F1 + 256:T_F1 + 384] = -S
    C, S = _stage1(128, 1.0 / np.sqrt(S2 * 128.0))
    tb[:, T_F1 + 384:T_F1 + 512] = C
    tb[:, T_F1 + 512:T_F1 + 640] = S
    tb[:, T_F1 + 640:T_F1 + 768] = -S
    kj = np.arange(128)[:, None]
    qi = np.arange(128)[None, :]
    mL = (kj >= qi).astype(np.float64)
    mR = (kj <= qi).astype(np.float64)
    tb[:, T_MASK:T_MASK + 128] = mL
    tb[:, T_MASK + 128:T_MASK + 256] = mR
    if not pair:
        tb[:, T_MASK + 256:T_MASK + 384] = mL
        tb[:, T_MASK + 384:T_MASK + 512] = mR
    t2 = np.concatenate([_stage2(S1, pair), _stage2(S2, False)], axis=0)
    pos = np.concatenate([
        (np.arange(S1) % seq1), np.arange(S2)]).astype(np.float32)
    inv = (np.float32(500000.0) ** (-np.arange(0, 16, 2, dtype=np.float32) / np.float32(16))).astype(np.float32)
    a = pos[:, None] * inv[None, :]
    rope = np.concatenate([np.cos(a), np.sin(a)], axis=1).astype(np.float32)
    rope = rope.reshape(NB, 128, 16)
    return tb.astype(BF), t2.astype(BF), np.ascontiguousarray(rope)


def build_program(S1, S2, DEPTH):
    NT = S1 + S2
    NB = NT // 128
    NB1 = S1 // 128
    units = [(0, S1), (S1, S2)]
    nc = bass.Bass("TRN2", target_bir_lowering=False)

    def din(name, shape, dt):
        return nc.dram_tensor(name, list(shape), dt, kind="ExternalInput").ap()

    xin = din("xin", [NT, D], F32)
    pin = din("pin", [DEPTH, NT, PLE], F32)
    w_in = din("w_in", [DEPTH, D, IN_W], F32)
    w_fmix = din("w_fmix", [DEPTH, 512, 128], F32)
    w_ao = din("w_ao", [DEPTH, 512, D], F32)
    w_fo = din("w_fo", [DEPTH, 512, D], F32)
    w_out = din("w_out", [DEPTH, D, D], F32)
    w_pe = din("w_pe", [DEPTH, PLE, D], F32)
    w_pg = din("w_pg", [DEPTH, D, D], F32)
    smallp = din("smallp", [128, DEPTH * 24], F32)
    lnfd = din("lnfb", [128, D], F32)
    tb16 = din("tb16", [128, TB16_W], BF16)
    t2d = din("t2", [NB, 128, 256], BF16)
    roped = din("rope", [NB, 128, 16], F32)
    yout = nc.dram_tensor("yout", [NT, D], F32, kind="ExternalOutput").ap()
    hbuf = nc.dram_tensor("hbuf", [NT, D], F32, kind="Internal").ap()
    pqs = nc.dram_tensor("pqs", [NT, D], BF16, kind="Internal").ap()
    ysc = nc.dram_tensor("ysc", [NT, D], BF16, kind="Internal").ap()
    fsc = nc.dram_tensor("fsc", [NT, 512], BF16, kind="Internal").ap()

    es = ExitStack()
    with es:
        P = Prog(nc, es)

        def sb(name, shape, dt):
            return es.enter_context(nc.sbuf_tensor(name, list(shape), dt))

        def ps(name, shape, dt):
            return es.enter_context(nc.psum_tensor(name, list(shape), dt))

        tb = sb("tb_sb", [128, TB16_W], BF16)
        s_tb = Slot("tb")
        ident = tb[:, T_ID:T_ID + 128]
        csT = tb[:, T_CS:T_CS + 256]
        smp = sb("smp_sb", [128, DEPTH * 24], F32)
        s_smp = Slot("smp")
        esink = sb("esink", [128, DEPTH * 8], F32)
        s_esink = Slot("esink")
        stats = sb("stats", [128, NB], F32)
        rstd1 = sb("rstd1", [128, NB], F32)
        rstd1h = sb("rstd1h", [128, NB], F32)
        s_stats = [Slot("stats%d" % b) for b in range(NB)]
        s_rstd1 = Slot("rstd1")
        d_const = P.dsem()

        P.op("sp", lambda e: [e.dma_start(out=tb[:], in_=tb16),
                              e.dma_start(out=smp[:], in_=smallp)],
             writes=[s_tb, s_smp], dsem=d_const, ndma=2)
        for l in range(DEPTH):
            P.op("act", lambda e, l=l: e.activation(out=esink[:, l * 8:(l + 1) * 8],
                                                     in_=smp[:, l * 24 + 16:l * 24 + 24], func=AF.Exp),
                 reads=[s_smp], writes=[s_esink])

        wb_in = sb("wb_in", [128, 8, IN_W], BF16)
        wb_ao = sb("wb_ao", [128, 4, D], BF16)
        wb_fo = sb("wb_fo", [128, 4, D], BF16)
        wb_out = sb("wb_out", [128, 8, D], BF16)
        wb_pg = sb("wb_pg", [128, 8, D], BF16)
        wb_pe = sb("wb_pe", [128, 2, D], BF16)
        csw = sb("csw", [128, 4, 256], BF16)
        s_csw = Slot("csw")
        s_w = Slot("weights")
        conv_eng = Ring(["dve", "act"])

        s_w1 = Slot("weights_uf")
        f2_dsems = []

        def weight_jobs(l, part):
            jobs = []
            if part == 1:
                for kc in range(8):
                    rows = slice(kc * 128, (kc + 1) * 128)
                    jobs.append((w_in[l, rows, C_UF:C_UF + 512], wb_in[:, kc, C_UF:C_UF + 512], 512, False, s_w1))
                for kc in range(4):
                    rows = slice(kc * 128, (kc + 1) * 128)
                    jobs.append((w_fmix[l, rows, :], junk[:, kc * 128:(kc + 1) * 128], 128, False, s_junk))
                return jobs
            for kc in range(8):
                rows = slice(kc * 128, (kc + 1) * 128)
                for c0 in list(range(0, C_UF, 512)) + list(range(C_UF + 512, IN_W, 512)):
                    c1 = min(c0 + 512, C_UF if c0 < C_UF else IN_W)
                    jobs.append((w_in[l, rows, c0:c1], wb_in[:, kc, c0:c1], c1 - c0, c0 == 0, s_w))
            for wd, wb_, nk in ((w_ao, wb_ao, 4), (w_fo, wb_fo, 4), (w_out, wb_out, 8), (w_pg, wb_pg, 8), (w_pe, wb_pe, 2)):
                for kc in range(nk):
                    rows = slice(kc * 128, (kc + 1) * 128)
                    for c0 in (0, 512):
                        jobs.append((wd[l, rows, c0:c0 + 512], wb_[:, kc, c0:c0 + 512], 512, False, s_w))
            return jobs

        def weight_job(job, stg):
            src, dst, n, permq, s_dst = job
            st, s_st, d_st = stg
            P.op("sp", lambda e: e.dma_start(out=st[:, 0:n], in_=src), writes=[s_st], dsem=d_st, cost=2.0)
            ce = conv_eng.next()

            def conv(e):
                cp = e.tensor_copy if ce == "dve" else e.copy
                if permq:
                    return cp(out=dst[:, 0:512].rearrange("p (i j d) -> p i j d", i=4, j=2),
                              in_=st[:, 0:512].rearrange("p (j i d) -> p i j d", i=4, j=2))
                return cp(out=dst, in_=st[:, 0:n])
            P.op(ce, conv, reads=[s_st], writes=[s_dst], n=n)

        def weights_part2_gen(l):
            if not f2_dsems:
                f2_dsems.extend(P.dsem() for _ in r_f2.items)
            stgs = Ring([(t_, s_, d_) for (t_, s_), d_ in zip(r_f2.items, f2_dsems)])
            for job in weight_jobs(l, 2):
                weight_job(job, stgs.next())
                yield

        def load_weights(l):
            for job in weight_jobs(l, 1):
                weight_job(job, r_h.next())
            for gp in range(2):
                pm, s_pm = r_pm.next()

                def ffold(e, pm=pm, gp=gp):
                    for gg in range(2):
                        g = gp * 2 + gg
                        for q in range(2):
                            i = e.matmul(pm[:, gg * 256 + q * 128:gg * 256 + (q + 1) * 128],
                                         lhsT=tb[:, T_CS + q * 128:T_CS + (q + 1) * 128],
                                         rhs=junk[:, g * 128:(g + 1) * 128], start=True, stop=True)
                    return i
                P.op("pe", ffold, reads=[s_tb, s_junk], writes=[s_pm])
                P.op("dve", lambda e, pm=pm, gp=gp: e.tensor_copy(
                    out=csw[:, gp * 2:gp * 2 + 2, :], in_=pm[:].rearrange("p (g c) -> p g c", g=2)),
                    reads=[s_pm], writes=[s_csw])

        def ring(prefix, n, shape, dt, dma=False, halves=False):
            items = []
            for i in range(n):
                t = sb("%s%d" % (prefix, i), shape, dt)
                sl = [Slot(prefix + "_lo"), Slot(prefix + "_hi")] if halves else Slot(prefix)
                if dma:
                    items.append((t, sl, P.dsem()))
                else:
                    items.append((t, sl))
            return Ring(items)

        r_h = ring("hslot", 4, [128, D], F32, dma=True, halves=True)
        r_hb = ring("hbs", 1, [128, D], BF16)
        r_hT = ring("hTs", 3, [128, D], BF16)
        r_fi = ring("fis", 3, [128, 512], BF16, dma=True)
        r_p = ring("pslot", 4, [128, PLE], F32, dma=True)
        r_rope = ring("ropes", 2, [128, 16], F32, dma=True)
        r_qT = ring("qT", 3, [128, 4, 128], BF16)
        kT = [(sb("kT%d" % i, [128, 128], BF16), Slot("kT")) for i in range(4)]
        vA = [(sb("vA%d" % i, [128, 2, 65], BF16), Slot("vA")) for i in range(4)]
        qb_t = sb("qb_t", [128, 640], BF16)
        s_qbq, s_qbk = Slot("qbq"), Slot("qbk")
        qr_t = sb("qr_t", [128, 10, 16], F32)
        s_qr = Slot("qr")
        ropeA = sb("ropeA", [128, 10, 16], F32)
        ropeT = sb("ropeT", [128, 10, 16], F32)
        s_ropeA, s_ropeT = Slot("ropeA"), Slot("ropeT")
        pT_t = sb("pT_t", [128, 3072], BF16)
        s_pTs = [[Slot("pT%d%d" % (k, g)) for g in range(2)] for k in range(3)]
        r_f2 = ring("f2k", 3, [128, 512], F32)
        r_b1 = ring("b1k", 5, [128, 512], BF16, dma=True)
        r_m = ring("mgs", 2, [128, D], BF16, dma=True, halves=True)
        r_h1b = ring("h1bs", 2, [128, D], BF16, dma=True, halves=True)
        r_h1T = ring("h1Ts", 1, [128, D], BF16, dma=True)
        r_b2 = Ring(r_m.items + r_h1b.items + r_h1T.items)
        r_tg = ring("tgs", 1, [128, 512], F32)
        r_pb = ring("pbs", 2, [128, 256], BF16)
        r_big = ring("big", 2, [128, 2048], BF16, dma=True)
        r_t2 = ring("t2s", 2, [128, 256], BF16, dma=True)
        r_den = ring("den", 2, [128, 16], F32)
        r_sm = ring("sm", 2, [128, 16], F32)
        junk = sb("junk", [128, D], BF16)
        two_k = list(r_b2.items) + [(pT_t[:, c * 1024:(c + 1) * 1024], s_pTs[c], P.dsem()) for c in range(3)]
        hb_items = list(r_hb.items) + list(r_hT.items)
        s_junk = Slot("junk")

        r_pm = Ring([(ps("pm%d" % i, [128, 512], F32), Slot("pm", True)) for i in range(7)])
        r_pt = Ring([(ps("pt%d" % i, [128, 8, 128], BF16), Slot("pt", True)) for i in range(1)])

        for i in range(4):
            P.op("dve", lambda e, i=i: e.memset(vA[i][0][:, :, 64:65], 1.0), writes=[vA[i][1]])

        s_hb = [Slot("hb%d" % b) for b in range(NB)]
        s_pq = [Slot("pq%d" % u) for u in range(2)]
        s_ys = [Slot("ys%d" % u) for u in range(2)]
        s_fs = [Slot("fs%d" % u) for u in range(2)]

        evac_eng = Ring(["act", "dve"])

        def copy_on(eng_name):
            if eng_name == "dve":
                return lambda e, o, i: e.tensor_copy(out=o, in_=i)
            return lambda e, o, i: e.copy(out=o, in_=i)

        def transpose_to(src_ap, s_src, n, dst_ap, s_dst, gcol=None, extra_reads=(), eng=None):
            pt, s_pt = r_pt.next()

            def f(e):
                for k in range(n):
                    i = e.transpose(out=pt[:, k, :], in_=src_ap[:, k * 128:(k + 1) * 128], identity=ident)
                return i
            P.op("pe", f, reads=[s_src, s_tb], writes=[s_pt], cost=0.13 * n)
            if gcol is not None:
                P.op("dve", lambda e: e.tensor_tensor(
                    out=dst_ap, in0=pt[:, 0:n, :], in1=gcol.unsqueeze(2).to_broadcast([128, n, 128]), op=ALU.mult),
                    reads=[s_pt, s_smp] + list(extra_reads), writes=[s_dst], n=128 * n)
            else:
                en = eng or evac_eng.next()
                P.op(en, lambda e: copy_on(en)(e, dst_ap, pt[:, 0:n, :]), reads=[s_pt] + list(extra_reads), writes=[s_dst],
                     n=128 * n)

        def batch_rstd():
            P.op("dve", lambda e: e.tensor_scalar(out=rstd1[:], in0=stats[:], scalar1=1.0 / D, scalar2=EPS,
                                                   op0=ALU.mult, op1=ALU.add),
                 reads=s_stats, writes=[s_rstd1])
            P.op("act", lambda e: e.activation(out=rstd1[:], in_=rstd1[:], func=AF.Sqrt), reads=[s_rstd1], writes=[s_rstd1])
            P.op("dve", lambda e: e.reciprocal(out=rstd1[:], in_=rstd1[:]), reads=[s_rstd1], writes=[s_rstd1])
            P.op("dve", lambda e: e.tensor_scalar(out=rstd1h[:], in0=rstd1[:], scalar1=0.5, scalar2=None, op0=ALU.mult),
                 reads=[s_rstd1], writes=[s_rstd1])

        def src_rows(l, b):
            t = xin if l == 0 else hbuf
            return t[b * 128:(b + 1) * 128, :]

        for b in range(NB):
            ht, s_ht, d_ht = r_h.next()
            P.op("sp", lambda e, ht=ht, b=b: e.dma_start(out=ht[:], in_=src_rows(0, b)), writes=[s_ht], dsem=d_ht)
            P.op("act", lambda e, ht=ht, b=b: e.activation(out=junk[:], in_=ht[:], func=AF.Square,
                                                            accum_out=stats[:, b:b + 1]),
                 reads=[s_ht], writes=[s_junk, s_stats[b]])

        def do_layer(l):
            ckpt(1)
            load_weights(l)
            batch_rstd()
            ckpt(2)
            g1 = smp[:, l * 24:l * 24 + 8]
            gpg = smp[:, l * 24 + 8:l * 24 + 16]
            es_l = esink[:, l * 8:(l + 1) * 8]
            last = (l == DEPTH - 1)

            def do_unit(u, T0, SU):
                B0 = T0 // 128
                nbu = SU // 128
                N2 = SU // 128
                G = 128 // N2
                f1 = T_F1 + 384 * u
                C1 = tb[:, f1:f1 + 128]
                S1m = tb[:, f1 + 128:f1 + 256]
                nS1 = tb[:, f1 + 256:f1 + 384]

                SB_ = 2
                TW = SB_ * 128

                def schedule(threads):
                    threads = list(threads)
                    while threads:
                        progressed = False
                        for t in sorted(threads, key=lambda t: (t[1], t[2])):
                            P.step_fin = 0.0
                            try:
                                r = next(t[0])
                            except StopIteration:
                                threads.remove(t)
                                progressed = True
                                break
                            if r == "blocked":
                                continue
                            if P.step_fin > 0.0:
                                t[1] = P.step_fin
                            progressed = True
                            break
                        assert progressed, "schedule deadlock"

                def worker(queue, fn, w):
                    while queue:
                        item = queue.pop(0)
                        yield from fn(item, w)

                p1_rings = [dict(h=Ring(r_h.items[2 * w:2 * w + 2]), hb=Ring(hb_items[2 * w:2 * w + 2]),
                                 big=r_big.items[w], k2=Ring(two_k[4 * w:4 * w + 4])) for w in range(2)]

                def p1_supertile(st4, w):
                    R = p1_rings[w]
                    hT, s_hT, _ = R["big"]
                    hTv = hT[:, 0:8 * TW].rearrange("p (k t) -> p k t", k=8)
                    for j in range(SB_):
                        b = B0 + st4 * SB_ + j
                        ht, s_ht, d_ht = R["h"].next()
                        P.op("sp", lambda e, ht=ht, b=b: e.dma_start(out=ht[:], in_=src_rows(l, b)),
                             reads=[s_hb[b]], writes=[s_ht], dsem=d_ht)
                        yield
                        hb, s_hbf = R["hb"].next()[:2]
                        P.op("act", lambda e, hb=hb, ht=ht: e.copy(out=hb[:], in_=ht[:]), reads=[s_ht], writes=[s_hbf], n=1024)
                        yield
                        transpose_to(hb, s_hbf, 8, hTv[:, :, j * 128:(j + 1) * 128], s_hT, gcol=g1)
                        yield
                    ufT, s_ufT, _ = R["k2"].next()
                    ufv = ufT[:, 0:4 * TW].rearrange("p (g t) -> p g t", g=4)
                    for gp in range(2):
                        pm, s_pm = r_pm.next()

                        def f(e, pm=pm, gp=gp, hTv=hTv):
                            for gg in range(2):
                                g = gp * 2 + gg
                                for kc in range(8):
                                    i = e.matmul(pm[:, gg * TW:(gg + 1) * TW],
                                                 lhsT=wb_in[:, kc, C_UF + g * 128:C_UF + (g + 1) * 128],
                                                 rhs=hTv[:, kc, :], start=(kc == 0), stop=(kc == 7))
                            return i
                        P.op("pe", f, reads=[s_hT, s_w1], writes=[s_pm], cost=3.0)
                        en = evac_eng.next()
                        P.op(en, lambda e, en=en, gp=gp, pm=pm, ufv=ufv: copy_on(en)(
                            e, ufv[:, gp * 2:gp * 2 + 2, :], pm[:, 0:2 * TW].rearrange("p (g t) -> p g t", g=2)),
                            reads=[s_pm], writes=[s_ufT])
                        yield
                    for j in range(SB_):
                        b = B0 + st4 * SB_ + j
                        pqb, s_pqb, d_pqb = R["k2"].next()
                        pqv = pqb[:].rearrange("p (q g c) -> p q g c", q=2, g=4)
                        for half in range(2):
                            pm, s_pm = r_pm.next()

                            def f(e, pm=pm, half=half, j=j, ufv=ufv):
                                for gg in range(2):
                                    g = half * 2 + gg
                                    i = e.matmul(pm[:, gg * 256:(gg + 1) * 256],
                                                 lhsT=ufv[:, g, j * 128:(j + 1) * 128],
                                                 rhs=csw[:, g, :], start=True, stop=True)
                                return i
                            P.op("pe", f, reads=[s_ufT, s_csw], writes=[s_pm], cost=0.4)
                            P.op("act", lambda e, pm=pm, half=half, pqv=pqv, b=b: e.activation(
                                out=pqv[:, :, half * 2:half * 2 + 2, :].rearrange("p q g c -> p g q c"),
                                in_=pm[:].rearrange("p (g q c) -> p g q c", g=2, q=2),
                                func=AF.Copy, scale=rstd1[:, b:b + 1]),
                                reads=[s_pm, s_rstd1], writes=[s_pqb])
                        P.op("pool", lambda e, pqb=pqb, b=b: e.dma_start(out=pqs[b * 128:(b + 1) * 128, :], in_=pqb[:]),
                             reads=[s_pqb], writes=[s_pq[u]], dsem=d_pqb)
                        yield

                q1 = list(range(nbu // SB_))
                schedule([[worker(q1, p1_supertile, w), 0.0, w] for w in range(2)]
                         + ([[weights_part2_gen(l), 0.0, 5]] if u == 0 else []))

                ckpt(3)
                pq_v = pqs[T0:T0 + SU, :].rearrange("(a n) c -> a n c", n=N2)
                ys_v = ysc[T0:T0 + SU, :].rearrange("(a n) c -> a n c", n=N2)

                def f1_item(n2, w):
                    X, s_X, d_X = two_k[2 * w]
                    Y, s_Y, d_Y = two_k[2 * w + 1]
                    P.op("sp", lambda e: e.dma_start(out=X[:, 0:1024], in_=pq_v[:, n2, :]),
                         reads=[s_pq[u]], writes=[s_X], dsem=d_X)
                    yield
                    pmP, s_pmP = r_pm.next()
                    pmQ, s_pmQ = r_pm.next()

                    def f(e):
                        e.matmul(pmP[:], lhsT=C1, rhs=X[:, 0:512], start=True, stop=False)
                        e.matmul(pmP[:], lhsT=nS1, rhs=X[:, 512:1024], start=False, stop=True)
                        e.matmul(pmQ[:], lhsT=C1, rhs=X[:, 512:1024], start=True, stop=False)
                        return e.matmul(pmQ[:], lhsT=S1m, rhs=X[:, 0:512], start=False, stop=True)
                    P.op("pe", f, reads=[s_X, s_tb], writes=[s_pmP, s_pmQ], cost=1.6)
                    P.op("act", lambda e: e.copy(out=Y[:, 0:512], in_=pmP[:]), reads=[s_pmP], writes=[s_Y])
                    P.op("dve", lambda e: e.tensor_copy(out=Y[:, 512:1024], in_=pmQ[:]), reads=[s_pmQ], writes=[s_Y])
                    P.op("pool", lambda e: e.dma_start(out=ys_v[:, n2, :], in_=Y[:, 0:1024]),
                         reads=[s_Y], writes=[s_ys[u]], dsem=d_Y)
                    yield

                qf1 = list(range(N2))
                schedule([[worker(qf1, f1_item, w), 0.0, w] for w in range(4)])

                ckpt(4)
                ysu = ysc[T0:T0 + SU, :].rearrange("(k n) c -> k n c", n=N2)
                fsu = fsc[T0:T0 + SU, :].rearrange("(m a) c -> a m c", a=N2)
                f2_rings = [dict(k2=Ring(two_k[4 * w:4 * w + 4]), fo=Ring(r_b1.items[2 * w:2 * w + 2]),
                                 t2=r_t2.items[w]) for w in range(2)]

                def f2_item(a, w):
                    R = f2_rings[w]
                    Yt, s_Yt, d_Yt = R["k2"].next()

                    def ld(e):
                        return [e.dma_start(out=Yt[j * N2:(j + 1) * N2, :], in_=ysu[a + N2 * j, :, :]) for j in range(G)]
                    P.op("sp", ld, reads=[s_ys[u]], writes=[s_Yt], dsem=d_Yt, ndma=G)
                    tt, s_tt, d_tt = R["t2"]
                    gi = (0 if u == 0 else NB1) + a
                    P.op("sp", lambda e: e.dma_start(out=tt[:], in_=t2d[gi, :, :]), writes=[s_tt], dsem=d_tt)
                    yield
                    pm, s_pm = r_pm.next()

                    def f(e):
                        e.matmul(pm[:], lhsT=tt[:, 0:128], rhs=Yt[:, 0:512], start=True, stop=False)
                        return e.matmul(pm[:], lhsT=tt[:, 128:256], rhs=Yt[:, 512:1024], start=False, stop=True)
                    P.op("pe", f, reads=[s_tt, s_Yt], writes=[s_pm], cost=0.8)
                    fo, s_fo, d_fo = R["fo"].next()
                    en = evac_eng.next()
                    P.op(en, lambda e: copy_on(en)(e, fo[:], pm[:]), reads=[s_pm], writes=[s_fo])
                    P.op("pool", lambda e: e.dma_start(out=fsu[a, :, :], in_=fo[:]),
                         reads=[s_fo], writes=[s_fs[u]], dsem=d_fo)
                    yield

                qf2 = list(range(N2))
                schedule([[worker(qf2, f2_item, w), 0.0, w] for w in range(2)])

                ckpt(5)
                st = {}
                mid = (u == 0 and NB1 % 2 == 0)

                def proj512(hTv, s_hT, c0):
                    pm, s_pm = r_pm.next()

                    def f(e):
                        for kc in range(8):
                            i = e.matmul(pm[:], lhsT=hTv[:, kc, :], rhs=wb_in[:, kc, c0:c0 + 512],
                                         start=(kc == 0), stop=(kc == 7))
                        return i
                    P.op("pe", f, reads=[s_hT, s_w], writes=[s_pm], cost=2.6)
                    return pm, s_pm

                def gen_A(bl):
                    b = B0 + bl
                    rs = rstd1[:, b:b + 1]
                    ht, s_ht, d_ht = r_h.next()
                    P.op("sp", lambda e: e.dma_start(out=ht[:], in_=src_rows(l, b)), reads=[s_hb[b]], writes=[s_ht], dsem=d_ht)
                    pslot, s_p, d_p = r_p.next()
                    P.op("sp", lambda e: e.dma_start(out=pslot[:], in_=pin[l, b * 128:(b + 1) * 128, :]), writes=[s_p], dsem=d_p)
                    fi, s_fi, d_fi = r_fi.next()
                    P.op("sp", lambda e: e.dma_start(out=fi[:], in_=fsc[b * 128:(b + 1) * 128, :]),
                         reads=[s_fs[u]], writes=[s_fi], dsem=d_fi)
                    rp, s_rope, d_rp = r_rope.next()
                    P.op("sp", lambda e: e.dma_start(out=rp[:], in_=roped[b, :, :]), writes=[s_rope], dsem=d_rp)
                    yield
                    hb, s_hbf = r_hb.next()
                    P.op("act", lambda e: e.copy(out=hb[:], in_=ht[:]), reads=[s_ht], writes=[s_hbf], n=1024)
                    yield
                    hT, s_hT = r_hT.next()
                    hTv = hT[:].rearrange("p (k t) -> p k t", k=8)
                    transpose_to(hb, s_hbf, 8, hTv, s_hT, gcol=g1)
                    yield
                    pmq, s_pmq = r_pm.next()
                    pmk, s_pmk = r_pm.next()

                    def f(e):
                        for kc in range(8):
                            e.matmul(pmq[:], lhsT=hTv[:, kc, :], rhs=wb_in[:, kc, 0:512], start=(kc == 0), stop=(kc == 7))
                        for kc in range(8):
                            i = e.matmul(pmk[:, 0:256], lhsT=hTv[:, kc, :], rhs=wb_in[:, kc, 512:768],
                                         start=(kc == 0), stop=(kc == 7))
                        return i
                    P.op("pe", f, reads=[s_hT, s_w], writes=[s_pmq, s_pmk], cost=4.0)
                    P.op("dve", lambda e: e.tensor_scalar(
                        out=qr_t[:, 0:8, :], in0=pmq[:].rearrange("p (h d) -> p h d", d=64)[:, :, 0:16],
                        scalar1=rs, scalar2=None, op0=ALU.mult), reads=[s_pmq, s_rstd1], writes=[s_qr], n=128)
                    P.op("dve", lambda e: e.tensor_scalar(
                        out=qr_t[:, 8:10, :], in0=pmk[:, 0:128].rearrange("p (h d) -> p h d", d=64)[:, :, 0:16],
                        scalar1=rs, scalar2=None, op0=ALU.mult), reads=[s_pmk, s_rstd1], writes=[s_qr], n=32)
                    P.op("act", lambda e: e.activation(out=qb_t[:, 0:512], in_=pmq[:], func=AF.Copy, scale=rs),
                         reads=[s_pmq, s_rstd1], writes=[s_qbq])
                    P.op("act", lambda e: e.activation(out=qb_t[:, 512:640], in_=pmk[:, 0:128], func=AF.Copy, scale=rs),
                         reads=[s_pmk, s_rstd1], writes=[s_qbk], n=128)
                    va, s_va = vA[bl % 4]
                    P.op("act", lambda e: e.activation(out=va[:, :, 0:64], in_=pmk[:, 128:256].rearrange("p (g d) -> p g d", g=2),
                                                       func=AF.Copy, scale=rs),
                         reads=[s_pmk, s_rstd1], writes=[s_va], n=128)
                    yield
                    qr4 = qr_t[:].rearrange("p h (t d) -> p h t d", t=2)
                    rA4 = ropeA[:].rearrange("p h (t d) -> p h t d", t=2)
                    rT4 = ropeT[:].rearrange("p h (t d) -> p h t d", t=2)
                    cosb = rp[:, 0:8].unsqueeze(1).unsqueeze(1).to_broadcast([128, 10, 2, 8])
                    sinb = rp[:, 8:16].unsqueeze(1).unsqueeze(1).to_broadcast([128, 10, 2, 8])
                    P.op("dve", lambda e: e.tensor_tensor(out=rA4, in0=qr4, in1=cosb, op=ALU.mult),
                         reads=[s_qr, s_rope], writes=[s_ropeA], n=160)
                    P.op("dve", lambda e: e.tensor_tensor(out=rT4, in0=qr4, in1=sinb, op=ALU.mult),
                         reads=[s_qr, s_rope], writes=[s_ropeT], n=160)
                    qbv = qb_t[:].rearrange("p (h d) -> p h d", d=64)
                    P.op("dve", lambda e: e.tensor_tensor(out=qbv[:, :, 0:8], in0=rA4[:, :, 0, :], in1=rT4[:, :, 1, :], op=ALU.subtract),
                         reads=[s_ropeA, s_ropeT], writes=[s_qbq, s_qbk], n=80)
                    P.op("dve", lambda e: e.tensor_tensor(out=qbv[:, :, 8:16], in0=rA4[:, :, 1, :], in1=rT4[:, :, 0, :], op=ALU.add),
                         reads=[s_ropeA, s_ropeT], writes=[s_qbq, s_qbk], n=80)
                    yield
                    qT, s_qT = r_qT.next()
                    kt, s_kt = kT[bl % 4]
                    pt, s_pt = r_pt.next()

                    def ft(e):
                        for k in range(5):
                            i = e.transpose(out=pt[:, k, :], in_=qb_t[:, k * 128:(k + 1) * 128], identity=ident)
                        return i
                    P.op("pe", ft, reads=[s_qbq, s_qbk, s_tb], writes=[s_pt], cost=0.65)
                    P.op("act", lambda e: e.copy(out=qT[:], in_=pt[:, 0:4, :]), reads=[s_pt], writes=[s_qT])
                    P.op("act", lambda e: e.copy(out=kt[:], in_=pt[:, 4, :]), reads=[s_pt], writes=[s_kt], n=128)
                    st[bl] = dict(ht=ht, s_ht=s_ht, d_ht=d_ht, hTv=hTv, s_hT=s_hT, qT=qT, s_qT=s_qT,
                                  pslot=pslot, s_p=s_p, fi=fi, s_fi=s_fi)

                def gen_B1(bl):
                    b = B0 + bl
                    S_ = st[bl]
                    ht, s_ht, hTv, s_hT = S_["ht"], S_["s_ht"], S_["hTv"], S_["s_hT"]
                    qT, s_qT = S_["qT"], S_["s_qT"]
                    rs = rstd1[:, b:b + 1]
                    rsh = rstd1h[:, b:b + 1]
                    kbs = [kb for kb in (-1, 0, 1) if 0 <= bl + kb < nbu]
                    pTv = pT_t[:].rearrange("p (k g c) -> p k g c", k=3, g=2)
                    for kb in kbs:
                        kt, s_kt = kT[(bl + kb) % 4]
                        for g in range(2):
                            pm, s_pm = r_pm.next()
                            P.op("pe", lambda e, pm=pm, kt=kt, g=g: e.matmul(
                                pm[:], lhsT=kt[64 * g:64 * g + 64, :], rhs=qT[64 * g:64 * g + 64, :, :].rearrange("p i t -> p (i t)"),
                                start=True, stop=True), reads=[s_kt, s_qT], writes=[s_pm], cost=0.3)
                            P.op("act", lambda e, pm=pm, kb=kb, g=g: e.activation(
                                out=pTv[:, kb + 1, g, :], in_=pm[:], func=AF.Exp, scale=0.125),
                                reads=[s_pm], writes=[s_pTs[kb + 1][g]])
                        if kb != 0:
                            mo = T_MASK + (0 if kb < 0 else 128)
                            if mid and ((kb < 0 and bl == NB1 // 2) or (kb > 0 and bl == NB1 // 2 - 1)):
                                mo += 256
                            mk = tb[:, mo:mo + 128].unsqueeze(1).to_broadcast([128, 8, 128])
                            P.op("dve", lambda e, kb=kb, mk=mk: e.tensor_tensor(
                                out=pTv[:, kb + 1, :, :].rearrange("p g (i t) -> p (g i) t", i=4),
                                in0=pTv[:, kb + 1, :, :].rearrange("p g (i t) -> p (g i) t", i=4), in1=mk, op=ALU.mult),
                                reads=[s_tb] + s_pTs[kb + 1], writes=s_pTs[kb + 1], n=1024)
                    yield
                    pga, s_pga = proj512(hTv, s_hT, C_GA)
                    sg, s_sg = r_f2.next()
                    P.op("act", lambda e: e.activation(out=sg[:], in_=pga[:], func=AF.Tanh, scale=rsh),
                         reads=[s_pga, s_rstd1], writes=[s_sg])
                    P.op("dve", lambda e: e.scalar_tensor_tensor(out=sg[:], in0=sg[:], scalar=1.0, in1=pga[:],
                                                                 op0=ALU.add, op1=ALU.mult),
                         reads=[s_sg, s_pga], writes=[s_sg])
                    yield
                    pgf, s_pgf = proj512(hTv, s_hT, C_GF)
                    sgf, s_sgf = r_f2.next()
                    P.op("act", lambda e: e.activation(out=sgf[:], in_=pgf[:], func=AF.Tanh, scale=rsh),
                         reads=[s_pgf, s_rstd1], writes=[s_sgf])
                    P.op("dve", lambda e: e.scalar_tensor_tensor(out=sgf[:], in0=sgf[:], scalar=1.0, in1=pgf[:],
                                                                 op0=ALU.add, op1=ALU.mult),
                         reads=[s_sgf, s_pgf], writes=[s_sgf])
                    yield
                    fi, s_fi = S_["fi"], S_["s_fi"]
                    yb, s_yb, _ = r_b1.next()
                    P.op("dve", lambda e: e.scalar_tensor_tensor(out=yb[:], in0=fi[:], scalar=rs, in1=sgf[:],
                                                                 op0=ALU.mult, op1=ALU.mult),
                         reads=[s_fi, s_sgf, s_rstd1], writes=[s_yb])
                    yield
                    yT, s_yT, _ = r_b1.next()
                    yTv = yT[:].rearrange("p (g t) -> p g t", g=4)
                    transpose_to(yb, s_yb, 4, yTv, s_yT)
                    yield
                    po = [r_pm.next(), r_pm.next()]

                    def fpv(e):
                        for h in range(8):
                            g, i_ = h // 4, h % 4
                            pm = po[h // 4][0]
                            for n_, kb in enumerate(kbs):
                                va, _ = vA[(bl + kb) % 4]
                                ins = e.matmul(pm[:, i_ * 72:i_ * 72 + 65], lhsT=pTv[:, kb + 1, g, i_ * 128:(i_ + 1) * 128],
                                               rhs=va[:, g, :], start=(n_ == 0), stop=(n_ == len(kbs) - 1))
                        return ins
                    P.op("pe", fpv, reads=[x for kb in kbs for x in s_pTs[kb + 1]] + [vA[(bl + kb) % 4][1] for kb in kbs],
                         writes=[po[0][1], po[1][1]], cost=1.6)
                    den, s_den = r_den.next()
                    ab, s_ab, _ = r_b1.next()
                    at, s_at = r_f2.next()
                    for hh in range(2):
                        pm, s_pm = po[hh]
                        pv = pm[:, 0:288].rearrange("p (i c) -> p i c", i=4)
                        dsl = den[:, hh * 4:(hh + 1) * 4]
                        P.op("dve", lambda e, pv=pv, dsl=dsl, hh=hh: e.tensor_tensor(
                            out=dsl.unsqueeze(2), in0=pv[:, :, 64:65], in1=es_l[:, hh * 4:(hh + 1) * 4].unsqueeze(2), op=ALU.add),
                            reads=[s_pm, s_esink], writes=[s_den], n=4)
                        P.op("dve", lambda e, dsl=dsl: e.reciprocal(out=dsl, in_=dsl), reads=[s_den], writes=[s_den], n=4)
                        P.op("dve", lambda e, pv=pv, dsl=dsl, hh=hh: e.tensor_tensor(
                            out=at[:, hh * 256:(hh + 1) * 256].rearrange("p (i d) -> p i d", i=4),
                            in0=pv[:, :, 0:64], in1=dsl.unsqueeze(2).to_broadcast([128, 4, 64]), op=ALU.mult),
                            reads=[s_pm, s_den], writes=[s_at], n=256)
                    P.op("dve", lambda e: e.scalar_tensor_tensor(out=ab[:], in0=at[:], scalar=rs, in1=sg[:],
                                                                 op0=ALU.mult, op1=ALU.mult),
                         reads=[s_at, s_sg, s_rstd1], writes=[s_ab])
                    yield
                    mg, s_mg, _ = r_m.next()
                    aTv = None
                    for half in range(2):
                        cs_ = slice(half * 512, (half + 1) * 512)
                        pma, s_pma = proj512(hTv, s_hT, C_MGA + half * 512)
                        tma, s_tma = r_f2.next()
                        P.op("act", lambda e, tma=tma, pma=pma: e.activation(out=tma[:], in_=pma[:], func=AF.Tanh, scale=rsh),
                             reads=[s_pma, s_rstd1], writes=[s_tma])
                        yield
                        pmf, s_pmf = proj512(hTv, s_hT, C_MGF + half * 512)
                        tmf, s_tmf = r_f2.next()
                        P.op("act", lambda e, tmf=tmf, pmf=pmf: e.activation(out=tmf[:], in_=pmf[:], func=AF.Tanh, scale=rsh),
                             reads=[s_pmf, s_rstd1], writes=[s_tmf])
                        yield
                        if aTv is None:
                            aT, s_aT, _ = r_b1.next()
                            aTv = aT[:].rearrange("p (g t) -> p g t", g=4)
                            transpose_to(ab, s_ab, 4, aTv, s_aT)
                            yield
                        pA, s_pA = r_pm.next()
                        pF, s_pF = r_pm.next()

                        def fa(e, pA=pA, cs_=cs_):
                            for kc in range(4):
                                i = e.matmul(pA[:], lhsT=aTv[:, kc, :], rhs=wb_ao[:, kc, cs_], start=(kc == 0), stop=(kc == 3))
                            return i
                        P.op("pe", fa, reads=[s_aT, s_w], writes=[s_pA], cost=1.3)

                        def ff(e, pF=pF, cs_=cs_):
                            for kc in range(4):
                                i = e.matmul(pF[:], lhsT=yTv[:, kc, :], rhs=wb_fo[:, kc, cs_], start=(kc == 0), stop=(kc == 3))
                            return i
                        P.op("pe", ff, reads=[s_yT, s_w], writes=[s_pF], cost=1.3)
                        P.op("dve", lambda e, tma=tma, pA=pA: e.scalar_tensor_tensor(
                            out=tma[:], in0=tma[:], scalar=1.0, in1=pA[:], op0=ALU.add, op1=ALU.mult),
                            reads=[s_tma, s_pA], writes=[s_tma])
                        P.op("dve", lambda e, tmf=tmf, pF=pF: e.scalar_tensor_tensor(
                            out=tmf[:], in0=tmf[:], scalar=1.0, in1=pF[:], op0=ALU.add, op1=ALU.mult),
                            reads=[s_tmf, s_pF], writes=[s_tmf])
                        P.op("dve", lambda e, tma=tma, tmf=tmf, cs_=cs_: e.tensor_tensor(
                            out=mg[:, cs_], in0=tma[:], in1=tmf[:], op=ALU.add), reads=[s_tma, s_tmf], writes=[s_mg[half]])
                        yield
                    mT, s_mT, _ = r_m.next()
                    mTv = mT[:].rearrange("p (k t) -> p k t", k=8)
                    transpose_to(mg, s_mg, 8, mTv, s_mT)
                    yield
                    h1b, s_h1b, _ = r_h1b.next()
                    pOs = []
                    for half in range(2):
                        cs_ = slice(half * 512, (half + 1) * 512)
                        pO, s_pO = r_pm.next()

                        def fo_(e, pO=pO, cs_=cs_):
                            for kc in range(8):
                                i = e.matmul(pO[:], lhsT=mTv[:, kc, :], rhs=wb_out[:, kc, cs_], start=(kc == 0), stop=(kc == 7))
                            return i
                        P.op("pe", fo_, reads=[s_mT, s_w], writes=[s_pO], cost=2.6)
                        P.op("dve", lambda e, pO=pO, cs_=cs_: e.scalar_tensor_tensor(
                            out=h1b[:, cs_], in0=pO[:], scalar=0.25, in1=ht[:, cs_], op0=ALU.mult, op1=ALU.add),
                            reads=[s_pO, s_ht[half]], writes=[s_h1b[half]])
                        pOs.append((pO, s_pO, cs_, half))
                    for pO, s_pO, cs_, half in pOs:
                        P.op("dve", lambda e, pO=pO, cs_=cs_: e.scalar_tensor_tensor(
                            out=ht[:, cs_], in0=pO[:], scalar=0.25, in1=ht[:, cs_], op0=ALU.mult, op1=ALU.add),
                            reads=[s_pO, s_ht[half]], writes=[s_ht[half]])
                    S_["h1b"], S_["s_h1b"] = h1b, s_h1b

                def gen_B2(bl):
                    b = B0 + bl
                    S_ = st.pop(bl)
                    ht, s_ht, d_ht = S_["ht"], S_["s_ht"], S_["d_ht"]
                    h1b, s_h1b = S_["h1b"], S_["s_h1b"]
                    sm, s_sm = r_sm.next()
                    P.op("act", lambda e: e.activation(out=junk[:], in_=ht[:], func=AF.Square, accum_out=sm[:, 0:1]),
                         reads=[s_ht], writes=[s_junk, s_sm], n=1024)
                    h1T, s_h1T, _ = r_h1T.next()
                    h1Tv = h1T[:].rearrange("p (k t) -> p k t", k=8)
                    transpose_to(h1b, s_h1b, 8, h1Tv, s_h1T, gcol=gpg)
                    yield
                    pslot, s_p = S_["pslot"], S_["s_p"]
                    pb, s_pb = r_pb.next()
                    P.op("act", lambda e: e.copy(out=pb[:], in_=pslot[:]), reads=[s_p], writes=[s_pb], n=256)
                    P.op("dve", lambda e: e.tensor_scalar(out=sm[:, 1:2], in0=sm[:, 0:1], scalar1=1.0 / D, scalar2=EPS,
                                                          op0=ALU.mult, op1=ALU.add), reads=[s_sm], writes=[s_sm], n=1)
                    P.op("act", lambda e: e.activation(out=sm[:, 2:3], in_=sm[:, 1:2], func=AF.Sqrt), reads=[s_sm], writes=[s_sm], cost=2.9)
                    P.op("dve", lambda e: e.reciprocal(out=sm[:, 3:4], in_=sm[:, 2:3]), reads=[s_sm], writes=[s_sm], n=1)
                    P.op("dve", lambda e: e.tensor_scalar(out=sm[:, 4:5], in0=sm[:, 3:4], scalar1=0.5, scalar2=None, op0=ALU.mult),
                         reads=[s_sm], writes=[s_sm], n=1)
                    yield
                    pTt, s_pTt = r_pb.next()
                    pTtv = pTt[:].rearrange("p (k t) -> p k t", k=2)
                    transpose_to(pb, s_pb, 2, pTtv, s_pTt)
                    yield
                    for half in range(2):
                        cs_ = slice(half * 512, (half + 1) * 512)
                        pG, s_pG = r_pm.next()

                        def fg(e, pG=pG, cs_=cs_):
                            for kc in range(8):
                                i = e.matmul(pG[:], lhsT=h1Tv[:, kc, :], rhs=wb_pg[:, kc, cs_], start=(kc == 0), stop=(kc == 7))
                            return i
                        P.op("pe", fg, reads=[s_h1T, s_w], writes=[s_pG], cost=2.6)
                        tg, s_tg = r_tg.next()
                        P.op("act", lambda e, tg=tg, pG=pG: e.activation(out=tg[:], in_=pG[:], func=AF.Tanh, scale=sm[:, 4:5]),
                             reads=[s_pG, s_sm], writes=[s_tg])
                        pE, s_pE = r_pm.next()

                        def fe(e, pE=pE, cs_=cs_):
                            for kc in range(2):
                                i = e.matmul(pE[:], lhsT=pTtv[:, kc, :], rhs=wb_pe[:, kc, cs_], start=(kc == 0), stop=(kc == 1))
                            return i
                        P.op("pe", fe, reads=[s_pTt, s_w], writes=[s_pE], cost=0.65)
                        P.op("dve", lambda e, tg=tg, pE=pE: e.scalar_tensor_tensor(
                            out=tg[:], in0=tg[:], scalar=1.0, in1=pE[:], op0=ALU.add, op1=ALU.mult),
                            reads=[s_tg, s_pE], writes=[s_tg])
                        P.op("dve", lambda e, tg=tg, cs_=cs_: e.scalar_tensor_tensor(
                            out=ht[:, cs_], in0=tg[:], scalar=0.5, in1=ht[:, cs_], op0=ALU.mult, op1=ALU.add),
                            reads=[s_tg, s_ht[half]], writes=[s_ht[half]])
                        yield
                    P.op("act", lambda e: e.activation(out=junk[:], in_=ht[:], func=AF.Square, accum_out=stats[:, b:b + 1]),
                         reads=[s_ht], writes=[s_junk, s_stats[b]], n=1024)
                    P.op("pool", lambda e: e.dma_start(out=hbuf[b * 128:(b + 1) * 128, :], in_=ht[:]),
                         reads=[s_ht], writes=[s_hb[b]], dsem=d_ht)

                return gen_A, gen_B1, gen_B2, nbu, schedule

            ug = [do_unit(u, T0, SU) for u, (T0, SU) in enumerate(units)]
            seq = [(u, bl) for u in range(len(units)) for bl in range(ug[u][3])]
            ntot = len(seq)
            prog = {"A": 0, "B1": 0, "B2": 0}

            def role(name, idx, can_start):
                for J, (u, bl) in enumerate(seq):
                    while not can_start(J):
                        yield "blocked"
                    yield from ug[u][idx](bl)
                    prog[name] = J + 1

            roles = [
                [role("B1", 1, lambda j: prog["A"] >= min(j + 2, ntot) and prog["B2"] >= j - 1), 0.0, 0],
                [role("A", 0, lambda j: prog["B2"] >= j - 3 and prog["B1"] >= j - 2), 0.0, 1],
                [role("B2", 2, lambda j: prog["B1"] >= j + 1), 0.0, 2],
            ]
            ug[0][4](roles)

        try:
            for l in range(DEPTH):
                do_layer(l)
        except StopBuild:
            pass

        batch_rstd()
        lnf_t, s_lnf, d_lnf = r_h.items[0]
        lnf_b = lnf_t[:]
        r_hf = Ring(r_h.items[1:])
        P.op("sp", lambda e: e.dma_start(out=lnf_b, in_=lnfd), writes=[s_lnf], dsem=d_lnf)
        for b in range(NB):
            ht, s_ht, d_ht = r_hf.next()
            P.op("sp", lambda e, ht=ht, b=b: e.dma_start(out=ht[:], in_=hbuf[b * 128:(b + 1) * 128, :]),
                 reads=[s_hb[b]], writes=[s_ht], dsem=d_ht)
            P.op("dve", lambda e, ht=ht, b=b: e.scalar_tensor_tensor(
                out=ht[:], in0=ht[:], scalar=rstd1[:, b:b + 1], in1=lnf_b, op0=ALU.mult, op1=ALU.mult),
                reads=[s_ht, s_rstd1, s_lnf], writes=[s_ht])
            P.op("pool", lambda e, ht=ht, b=b: e.dma_start(out=yout[b * 128:(b + 1) * 128, :], in_=ht[:]),
                 reads=[s_ht], dsem=d_ht)

        import sys as _sys
        print("[build] sbuf bytes remaining per partition:", nc.sbuf_bytes_remaining, file=_sys.stderr)
        with nc.Block() as block:
            @block.sync
            def _(eng):
                P.replay("sp", eng)

            @block.scalar
            def _(eng):
                P.replay("act", eng)

            @block.vector
            def _(eng):
                P.replay("dve", eng)

            @block.tensor
            def _(eng):
                P.replay("pe", eng)

            @block.gpsimd
            def _(eng):
                P.replay("pool", eng, final_waits=True)
    return nc


def pack_small(ln1, ln_pg, sink, ln_f, DEPTH):
    sp = np.zeros((128, DEPTH * 24), np.float32)
    for l in range(DEPTH):
        sp[:, l * 24:l * 24 + 8] = ln1[l].reshape(8, 128).T
        sp[:, l * 24 + 8:l * 24 + 16] = ln_pg[l].reshape(8, 128).T
        sp[:, l * 24 + 16:l * 24 + 24] = np.broadcast_to(sink[l][None, :], (128, 8))
    return sp, np.ascontiguousarray(np.broadcast_to(ln_f[None, :], (128, D))).astype(np.float32)


def run_cores(core_inputs, S1, S2, DEPTH, weights):
    nc = build_program(S1, S2, DEPTH)
    tabs = {}
    in_maps = []
    smallp, lnfb = pack_small(weights["ln1"], weights["ln_pg"], weights["sink"], weights["ln_f"], DEPTH)
    shared = dict(
        w_in=np.ascontiguousarray(weights["w_in"]),
        w_fmix=np.ascontiguousarray(weights["w_fmix"].reshape(DEPTH, 512, 128)),
        w_ao=np.ascontiguousarray(weights["w_ao"]), w_fo=np.ascontiguousarray(weights["w_fo"]),
        w_out=np.ascontiguousarray(weights["w_out"]), w_pe=np.ascontiguousarray(weights["w_pe"]),
        w_pg=np.ascontiguousarray(weights["w_pg"]), smallp=smallp, lnfb=lnfb)
    for ci in core_inputs:
        pair = ci["pair"]
        if pair not in tabs:
            tabs[pair] = make_tables(S1, S2, pair)
        tb, t2, rope = tabs[pair]
        m = dict(shared)
        m.update(xin=np.ascontiguousarray(ci["x"]), pin=np.ascontiguousarray(ci["p"]), tb16=tb, t2=t2, rope=rope)
        in_maps.append(m)
    res = run_bass_kernel_spmd(nc, in_maps, core_ids=list(range(len(in_maps))))
    return [np.asarray(r["yout"]) for r in res.results]


def kernel(x_prompt, x_sample, p_prompt, p_sample, ln1, w_in, sink, w_fmix, w_ao, w_fo,
           w_out, w_pe, ln_pg, w_pg, ln_f):
    f = lambda a: np.asarray(a, dtype=np.float32)
    x_prompt, x_sample, p_prompt, p_sample = f(x_prompt), f(x_sample), f(p_prompt), f(p_sample)
    weights = dict(ln1=f(ln1), w_in=f(w_in), sink=f(sink), w_fmix=f(w_fmix), w_ao=f(w_ao), w_fo=f(w_fo),
                   w_out=f(w_out), w_pe=f(w_pe), ln_pg=f(ln_pg), w_pg=f(w_pg), ln_f=f(ln_f))
    DEPTH = w_in.shape[0]
    S1, S2 = 8192, 4096
    cores = []
    assign = []
    for c in range(4):
        x = np.concatenate([x_sample[c], x_prompt[c]], axis=0)
        p = np.concatenate([p_sample[:, c], p_prompt[:, c]], axis=1)
        cores.append(dict(x=x, p=p, pair=False))
        assign.append((("s", c), ("p", c)))
    for c in range(4):
        i0 = 4 + 3 * c
        x = np.concatenate([x_prompt[i0], x_prompt[i0 + 1], x_prompt[i0 + 2]], axis=0)
        p = np.concatenate([p_prompt[:, i0], p_prompt[:, i0 + 1], p_prompt[:, i0 + 2]], axis=1)
        cores.append(dict(x=x, p=p, pair=True))
        assign.append((("p", i0), ("p", i0 + 1), ("p", i0 + 2)))
    outs = run_cores(cores, S1, S2, DEPTH, weights)
    y_prompt = np.empty_like(x_prompt)
    y_sample = np.empty_like(x_sample)
    for c in range(8):
        o = outs[c]
        if c < 4:
            y_sample[c] = o[0:8192]
            y_prompt[c] = o[8192:12288]
        else:
            i0 = 4 + 3 * (c - 4)
            y_prompt[i0] = o[0:4096]
            y_prompt[i0 + 1] = o[4096:8192]
            y_prompt[i0 + 2] = o[8192:12288]
    return (y_prompt, y_sample)
```
